# Optimizing a Trainium2 kernel written in Bass

```python
import math
import jax, jax.numpy as jnp
from jax import lax
import numpy as np

D_MODEL = 1024
BATCH = 32
SEQ = 2048
DEPTH = 4

N_MIXERS = 2
RWKV_HEAD = 64
RWKV_HEADS = D_MODEL // RWKV_HEAD
DECAY_LORA = 64
AAA_LORA = 64
MV_LORA = 32
GATE_LORA = 160
GN_EPS = 64e-5
N_RWKV_BRANCHES = 6
ATT_HEADS = 16
QK_HEAD = 64
V_HEAD = 64
Q_LORA = 256
KV_LORA = 128
IDX_HEADS = 8
IDX_DIM = 64
TOPK_MAX = 256
Q_BLOCK = 128
DSA_IN = Q_LORA + KV_LORA + IDX_DIM + IDX_HEADS
REL_BUCKETS = 32
REL_MAX_DIST = 128
D_FF = 2816
CONV_W = 3
DN_ALPHA = (2 * DEPTH) ** 0.25
DN_BETA = (8 * DEPTH) ** -0.25
LN_EPS = 1e-5

kernel_name = "hybrid_rwkv7_dsa_convffn_deepnorm"


def layer_norm(x, g, b, eps=LN_EPS):
    xf = x.astype(jnp.float32)
    mu = xf.mean(-1, keepdims=True)
    var = jnp.square(xf - mu).mean(-1, keepdims=True)
    return ((xf - mu) * lax.rsqrt(var + eps) * g + b).astype(x.dtype)


def rms_norm(x, g, eps=1e-6):
    xf = x.astype(jnp.float32)
    return (xf * lax.rsqrt(jnp.square(xf).mean(-1, keepdims=True) + eps) * g).astype(x.dtype)


def token_shift(x):
    return jnp.pad(x, ((0, 0), (1, 0), (0, 0)))[:, :-1]


def t5_bucket(dist):
    n = jnp.maximum(dist, 0)
    max_exact = REL_BUCKETS // 2
    nf = jnp.maximum(n, 1).astype(jnp.float32)
    large = max_exact + (jnp.log(nf / max_exact) / math.log(REL_MAX_DIST / max_exact)
                         * (REL_BUCKETS - max_exact)).astype(jnp.int32)
    large = jnp.minimum(large, REL_BUCKETS - 1)
    return jnp.where(n < max_exact, n, large)


def _wkv7_step(state, inp):
    r_t, w_t, k_t, v_t, a_t, b_t = inp
    sa = jnp.einsum('bhvk,bhk->bhv', state, a_t)
    state = (state * w_t[:, :, None, :] + sa[..., None] * b_t[:, :, None, :]
             + v_t[..., None] * k_t[:, :, None, :])
    y = jnp.einsum('bhvk,bhk->bhv', state, r_t)
    return state, y


def rwkv7_time_mix(x, v_first, vres, mix, w_rkv, w0, w1, w2, a0, a1, a2, g1, g2,
                   k_k, k_a, r_k, lnx_g, lnx_b, w_o):
    B, S, D = x.shape
    H, N = RWKV_HEADS, RWKV_HEAD
    f32 = jnp.float32
    xx = token_shift(x) - x
    xm = x[None] + xx[None] * mix[:, None, None, :]
    r, k, v = jnp.einsum('nbsd,nde->nbse', xm[:3], w_rkv)
    xv, xw, xa, xg = xm[2], xm[3], xm[4], xm[5]
    w_log = -jax.nn.softplus(-(w0 + jnp.tanh(xw @ w1) @ w2)) - 0.5
    decay = jnp.exp(-jnp.exp(w_log.astype(f32)))
    if vres is None:
        v_first = v
    else:
        v0, v1, v2 = vres
        v = v + (v_first - v) * jax.nn.sigmoid(v0 + (xv @ v1) @ v2)
    a = jax.nn.sigmoid(a0 + (xa @ a1) @ a2)
    g = jax.nn.sigmoid(xg @ g1) @ g2
    heads = lambda t: t.reshape(B, S, H, N)
    kk = heads(k * k_k).astype(f32)
    kk = kk / jnp.maximum(jnp.sqrt(jnp.sum(kk * kk, -1, keepdims=True)), 1e-12)
    k = k * (1 + (a - 1) * k_a)
    rh, kh, vh, ah = heads(r), heads(k), heads(v), heads(a)
    tm = lambda t: jnp.moveaxis(t.astype(f32), 1, 0)
    xs = (tm(rh), tm(heads(decay)), tm(kh), tm(vh), tm(-kk), tm(kk * ah.astype(f32)))
    state0 = jnp.zeros((B, H, N, N), f32)
    _, y = lax.scan(_wkv7_step, state0, xs)
    y = jnp.moveaxis(y, 0, 1)
    mu = y.mean(-1, keepdims=True)
    var = jnp.square(y - mu).mean(-1, keepdims=True)
    y = ((y - mu) * lax.rsqrt(var + GN_EPS)).reshape(B, S, D) * lnx_g + lnx_b
    bonus = jnp.sum(rh * kh * r_k, -1, keepdims=True) * vh
    y = (y.astype(x.dtype) + bonus.reshape(B, S, D)) * g
    return y @ w_o, v_first


def dsa_attention(x, w_in, q_norm_g, kv_norm_g, w_uq, w_uk, w_uv, w_qidx, kidx_g, kidx_b,
                  rel_bias, w_o):
    B, S, D = x.shape
    H = ATT_HEADS
    topk = min(TOPK_MAX, S // 4)
    h = x @ w_in
    c_q, c_kv, k_idx, w_idx = jnp.split(
        h, [Q_LORA, Q_LORA + KV_LORA, Q_LORA + KV_LORA + IDX_DIM], axis=-1)
    c_q = rms_norm(c_q, q_norm_g)
    c_kv = rms_norm(c_kv, kv_norm_g)
    q = (c_q @ w_uq).reshape(B, S, H, QK_HEAD)
    q_idx = (c_q @ w_qidx).reshape(B, S, IDX_HEADS, IDX_DIM)
    k_idx = layer_norm(k_idx, kidx_g, kidx_b)
    w_idx = w_idx * (IDX_HEADS ** -0.5 * IDX_DIM ** -0.5)
    nb = S // Q_BLOCK
    blk = lambda t: jnp.swapaxes(t.reshape((B, nb, Q_BLOCK) + t.shape[2:]), 0, 1)
    starts = jnp.arange(nb, dtype=jnp.int32) * Q_BLOCK
    key_pos = jnp.arange(S, dtype=jnp.int32)
    qk_scale = QK_HEAD ** -0.5

    def block(args):
        q_b, qi_b, wi_b, t0 = args
        t_pos = t0 + jnp.arange(Q_BLOCK, dtype=jnp.int32)
        s_idx = jnp.einsum('bthd,bsd->bths', qi_b, k_idx)
        score = jnp.einsum('bths,bth->bts', jax.nn.relu(s_idx), wi_b).astype(jnp.float32)
        causal = key_pos[None, :] <= t_pos[:, None]
        score = jnp.where(causal[None], score, -jnp.inf)
        _, sel = lax.top_k(score, topk)
        kv_sel = jax.vmap(lambda c, i: c[i])(c_kv, sel)
        q_abs = jnp.einsum('bthd,hdc->bthc', q_b, w_uk)
        logits = jnp.einsum('bthc,btkc->bhtk', q_abs, kv_sel).astype(jnp.float32) * qk_scale
        dist = t_pos[None, :, None] - sel
        bias = rel_bias[t5_bucket(dist)].astype(jnp.float32)
        logits = logits + jnp.moveaxis(bias, -1, 1)
        logits = jnp.where((dist >= 0)[:, None], logits, -jnp.inf)
        p = jax.nn.softmax(logits, axis=-1).astype(x.dtype)
        o_lat = jnp.einsum('bhtk,btkc->bthc', p, kv_sel)
        o = jnp.einsum('bthc,hcv->bthv', o_lat, w_uv)
        return o.reshape(B, Q_BLOCK, H * V_HEAD)

    o = lax.map(block, (blk(q), blk(q_idx), blk(w_idx), starts))
    o = jnp.swapaxes(o, 0, 1).reshape(B, S, H * V_HEAD)
    return o @ w_o


def conv_ffn(x, w_up, conv_w, conv_b, w_down):
    S = x.shape[1]
    u = x @ w_up
    up = jnp.pad(u, ((0, 0), (CONV_W - 1, 0), (0, 0)))
    u = sum(up[:, j:j + S] * conv_w[j] for j in range(CONV_W)) + conv_b
    gate, val = jnp.split(u, 2, axis=-1)
    return (jax.nn.silu(gate) * val) @ w_down


def setup_inputs(seed: int = 0) -> dict:
    key = jax.random.key(seed)
    ks = iter(jax.random.split(key, 64))
    f32 = jnp.float32
    n = lambda shape, scale: scale * jax.random.normal(next(ks), shape, f32)
    D, H, N = D_MODEL, RWKV_HEADS, RWKV_HEAD
    n_rwkv = (DEPTH + 1) // 2
    n_dsa = DEPTH // 2
    n_vres = max(n_rwkv - 1, 0)
    return {
        "x": n((BATCH, SEQ, D), 1.0),
        "ln_g": 1.0 + n((DEPTH, 2, D), 0.02),
        "ln_b": n((DEPTH, 2, D), 0.02),
        "rwkv_mix": jax.random.uniform(next(ks), (n_rwkv, N_RWKV_BRANCHES, D), f32),
        "rwkv_w_rkv": n((n_rwkv, 3, D, D), D ** -0.5),
        "rwkv_w0": -1.5 + n((n_rwkv, D), 1.0),
        "rwkv_w1": n((n_rwkv, D, DECAY_LORA), D ** -0.5),
        "rwkv_w2": n((n_rwkv, DECAY_LORA, D), 0.5 * DECAY_LORA ** -0.5),
        "rwkv_a0": n((n_rwkv, D), 0.1),
        "rwkv_a1": n((n_rwkv, D, AAA_LORA), D ** -0.5),
        "rwkv_a2": n((n_rwkv, AAA_LORA, D), 0.5 * AAA_LORA ** -0.5),
        "rwkv_v0": n((n_vres, D), 0.1),
        "rwkv_v1": n((n_vres, D, MV_LORA), D ** -0.5),
        "rwkv_v2": n((n_vres, MV_LORA, D), 0.5 * MV_LORA ** -0.5),
        "rwkv_g1": n((n_rwkv, D, GATE_LORA), D ** -0.5),
        "rwkv_g2": n((n_rwkv, GATE_LORA, D), GATE_LORA ** -0.5),
        "rwkv_k_k": 0.85 + n((n_rwkv, D), 0.05),
        "rwkv_k_a": 1.0 + n((n_rwkv, D), 0.05),
        "rwkv_r_k": n((n_rwkv, H, N), 0.1),
        "rwkv_lnx_g": 1.0 + n((n_rwkv, D), 0.02),
        "rwkv_lnx_b": n((n_rwkv, D), 0.02),
        "rwkv_w_o": n((n_rwkv, D, D), DN_BETA * D ** -0.5),
        "dsa_w_in": n((n_dsa, D, DSA_IN), D ** -0.5),
        "dsa_q_norm_g": 1.0 + n((n_dsa, Q_LORA), 0.02),
        "dsa_kv_norm_g": 1.0 + n((n_dsa, KV_LORA), 0.02),
        "dsa_w_uq": n((n_dsa, Q_LORA, ATT_HEADS * QK_HEAD), Q_LORA ** -0.5),
        "dsa_w_uk": n((n_dsa, ATT_HEADS, QK_HEAD, KV_LORA), KV_LORA ** -0.5),
        "dsa_w_uv": n((n_dsa, ATT_HEADS, KV_LORA, V_HEAD), KV_LORA ** -0.5),
        "dsa_w_qidx": n((n_dsa, Q_LORA, IDX_HEADS * IDX_DIM), Q_LORA ** -0.5),
        "dsa_kidx_g": 1.0 + n((n_dsa, IDX_DIM), 0.02),
        "dsa_kidx_b": n((n_dsa, IDX_DIM), 0.02),
        "dsa_w_o": n((n_dsa, ATT_HEADS * V_HEAD, D), DN_BETA * (ATT_HEADS * V_HEAD) ** -0.5),
        "rel_bias": n((REL_BUCKETS, ATT_HEADS), 0.5),
        "ffn_w_up": n((DEPTH, D, 2 * D_FF), D ** -0.5),
        "ffn_conv_w": n((DEPTH, CONV_W, 2 * D_FF), CONV_W ** -0.5),
        "ffn_conv_b": n((DEPTH, 2 * D_FF), 0.02),
        "ffn_w_down": n((DEPTH, D_FF, D), DN_BETA * D_FF ** -0.5),
    }


def reference(x, ln_g, ln_b, rwkv_mix, rwkv_w_rkv, rwkv_w0, rwkv_w1, rwkv_w2, rwkv_a0, rwkv_a1,
              rwkv_a2, rwkv_v0, rwkv_v1, rwkv_v2, rwkv_g1, rwkv_g2, rwkv_k_k, rwkv_k_a, rwkv_r_k,
              rwkv_lnx_g, rwkv_lnx_b, rwkv_w_o, dsa_w_in, dsa_q_norm_g, dsa_kv_norm_g, dsa_w_uq,
              dsa_w_uk, dsa_w_uv, dsa_w_qidx, dsa_kidx_g, dsa_kidx_b, dsa_w_o, rel_bias,
              ffn_w_up, ffn_conv_w, ffn_conv_b, ffn_w_down):
    v_first = None
    for i in range(DEPTH):
        j = i // N_MIXERS
        if i % N_MIXERS == 0:
            vres = None if j == 0 else (rwkv_v0[j - 1], rwkv_v1[j - 1], rwkv_v2[j - 1])
            h, v_first = rwkv7_time_mix(
                x, v_first, vres, rwkv_mix[j], rwkv_w_rkv[j], rwkv_w0[j], rwkv_w1[j], rwkv_w2[j],
                rwkv_a0[j], rwkv_a1[j], rwkv_a2[j], rwkv_g1[j], rwkv_g2[j], rwkv_k_k[j],
                rwkv_k_a[j], rwkv_r_k[j], rwkv_lnx_g[j], rwkv_lnx_b[j], rwkv_w_o[j])
        else:
            h = dsa_attention(
                x, dsa_w_in[j], dsa_q_norm_g[j], dsa_kv_norm_g[j], dsa_w_uq[j], dsa_w_uk[j],
                dsa_w_uv[j], dsa_w_qidx[j], dsa_kidx_g[j], dsa_kidx_b[j], rel_bias, dsa_w_o[j])
        x = layer_norm(DN_ALPHA * x + h, ln_g[i, 0], ln_b[i, 0])
        f = conv_ffn(x, ffn_w_up[i], ffn_conv_w[i], ffn_conv_b[i], ffn_w_down[i])
        x = layer_norm(DN_ALPHA * x + f, ln_g[i, 1], ln_b[i, 1])
    return x
```

```python
import math
from contextlib import ExitStack

import numpy as np
import concourse.bass as bass
import concourse.mybir as mybir
from concourse.ap import AP
from concourse.bass_utils import run_bass_kernel_spmd

F32 = mybir.dt.float32
BF16 = mybir.dt.bfloat16
AF = mybir.ActivationFunctionType
ALU = mybir.AluOpType
AX = mybir.AxisListType

D = 1024
DEPTH = 4
DFF = 2816
NCH = 44
NFC = 22
DN_ALPHA = (2 * DEPTH) ** 0.25
LN_EPS = 1e-5
SEM_ROT = 30000


class Cfg:
    def __init__(self, S=2048, NSEQ=4, TB=512, layers=(0, 1, 2, 3), parts=("mix", "ffn")):
        self.S = S
        self.NSEQ = NSEQ
        self.TB = TB
        self.layers = tuple(layers)
        self.parts = tuple(parts)


class Buf:
    __slots__ = ("w", "r")

    def __init__(self):
        self.w = None
        self.r = {}


class Tile:
    def __init__(self, t, n=1):
        self.t = t
        self.bs = [Buf() for _ in range(n)]
        self.b = self.bs[0]


class Prog:
    def __init__(self, nc, es):
        self.nc = nc
        self.es = es
        self.E = {"pe": nc.tensor, "act": nc.scalar, "dve": nc.vector, "pool": nc.gpsimd, "sp": nc.sync}
        self.sems = {}
        self.cnt = {}
        self.cur = {}
        self.nsem = 0
        for k in ("pe", "act", "dve", "pool"):
            self._newsem(k)
        self.dmaq = {"sp": [], "pool": []}
        self.dmai = {"sp": 0, "pool": 0}
        for q in ("sp", "pool"):
            for i in range(8):
                slot = "d_%s%d" % (q, i)
                self._newsem(slot)
                self.dmaq[q].append(slot)
        self.seen = {e: {} for e in self.E}
        self.ninst = 0
        self.nalloc = 0

    def _newsem(self, slot):
        self.nsem += 1
        key = "%s#%d" % (slot, self.nsem)
        self.sems[key] = self.es.enter_context(self.nc.semaphore("s%d" % self.nsem))
        self.cnt[key] = 0
        self.cur[slot] = key
        return key

    def _wait(self, eng, ev):
        key, val = ev
        if self.seen[eng].get(key, 0) >= val:
            return
        self.E[eng].wait_ge(self.sems[key], val)
        self.seen[eng][key] = val
        self.ninst += 1

    def _deps(self, eng, reads, writes):
        mykey = self.cur.get(eng)
        for b in reads:
            if b.w is not None:
                if eng == "pe" and b.w[0] == mykey:
                    continue
                self._wait(eng, b.w)
        for b in writes:
            if b.w is not None and not (eng == "pe" and b.w[0] == mykey):
                self._wait(eng, b.w)
            for k, v in b.r.items():
                if k == mykey:
                    continue
                self._wait(eng, (k, v))

    def _mark(self, ev, reads, writes):
        for b in reads:
            if b.r.get(ev[0], 0) < ev[1]:
                b.r[ev[0]] = ev[1]
        for b in writes:
            b.w = ev
            b.r = {}

    def op(self, eng, fn, reads=(), writes=()):
        self._deps(eng, reads, writes)
        key = self.cur[eng]
        if self.cnt[key] >= SEM_ROT:
            key = self._newsem(eng)
        ins = fn(self.E[eng])
        self.cnt[key] += 1
        ins.then_inc(self.sems[key], 1)
        self.ninst += 1
        self._mark((key, self.cnt[key]), reads, writes)

    def dma(self, q, out, in_, reads=(), writes=(), **kw):
        self._deps(q, reads, writes)
        i = self.dmai[q]
        self.dmai[q] = (i + 1) % len(self.dmaq[q])
        slot = self.dmaq[q][i]
        key = self.cur[slot]
        self._wait(q, (key, self.cnt[key]))
        if self.cnt[key] >= SEM_ROT:
            key = self._newsem(slot)
        ins = self.E[q].dma_start(out=out, in_=in_, **kw)
        self.cnt[key] += 16
        ins.then_inc(self.sems[key], 16)
        self.ninst += 1
        self._mark((key, self.cnt[key]), reads, writes)

    def barrier(self, engines=("pe", "act", "dve", "pool", "sp")):
        evs = [(k, v) for k, v in self.cnt.items() if v > 0]
        for e in engines:
            for ev in evs:
                if ev[0] == self.cur.get(e):
                    continue
                self._wait(e, ev)

    def sb(self, st, name, shape, dt, n=1):
        self.nalloc += 1
        return Tile(st.enter_context(self.nc.sbuf_tensor("sb%d_%s" % (self.nalloc, name), shape, dt)), n)

    def ps(self, st, name, shape=(128, 512), dt=F32, n=1):
        self.nalloc += 1
        return Tile(st.enter_context(self.nc.psum_tensor("ps%d_%s" % (self.nalloc, name), list(shape), dt)), n)


def bcast_rows(ap1d_tensor, offset, n, parts=128):
    return AP(ap1d_tensor, offset, [[0, parts], [1, n]])


def layer_norm_rows(P, Y, yb, stats, mv, rstd, g_bc, b_bc, out_eng="pool"):
    P.op("dve", lambda e: e.bn_stats(out=stats.t[:, 0, :], in_=Y[:, 0:512]), reads=[yb], writes=[stats.b])
    P.op("dve", lambda e: e.bn_stats(out=stats.t[:, 1, :], in_=Y[:, 512:1024]), reads=[yb], writes=[stats.b])
    P.op("dve", lambda e: e.bn_aggr(out=mv.t[:, :], in_=stats.t[:, :, :].rearrange("p a b -> p (a b)")),
         reads=[stats.b], writes=[mv.b])
    P.op("act", lambda e: e.activation(out=rstd.t[:, :], in_=mv.t[:, 1:2], func=AF.Sqrt, bias=LN_EPS, scale=1.0),
         reads=[mv.b], writes=[rstd.b])
    P.op("dve", lambda e: e.reciprocal(out=rstd.t[:, :], in_=rstd.t[:, :]), reads=[rstd.b], writes=[rstd.b])
    P.op("dve", lambda e: e.tensor_scalar(out=Y, in0=Y, scalar1=mv.t[:, 0:1], scalar2=rstd.t[:, 0:1],
                                          op0=ALU.subtract, op1=ALU.mult), reads=[yb, mv.b, rstd.b], writes=[yb])
    P.op(out_eng, lambda e: e.tensor_tensor(out=Y, in0=Y, in1=g_bc.t[:, :], op=ALU.mult), reads=[yb, g_bc.b], writes=[yb])
    P.op(out_eng, lambda e: e.tensor_tensor(out=Y, in0=Y, in1=b_bc.t[:, :], op=ALU.add), reads=[yb, b_bc.b], writes=[yb])


def load_weight_bf16(P, dst, dstb, src_t, src_off, K, N, kchunks):
    CB = 1024
    for kc in range(kchunks):
        rows = min(128, K - kc * 128)
        for c0 in range(0, N, CB):
            c1 = min(N, c0 + CB)
            src = AP(src_t, src_off + kc * 128 * N + c0, [[N, rows], [1, c1 - c0]])
            P.dma("pool", dst[0:rows, kc, c0:c1], src, writes=[dstb])


def ffn_phase(P, cfg, li, xin, xout, T):
    nc = P.nc
    S, NSEQ, TB = cfg.S, cfg.NSEQ, cfg.TB
    NT = TB // 128
    with ExitStack() as ph:
        wup = P.sb(ph, "wup", [128, 8, 2 * DFF], BF16)
        wd = P.sb(ph, "wd", [128, NFC, D], BF16)
        X = P.sb(ph, "X", [128, NT, D], F32, n=NT)
        xT = P.sb(ph, "xT", [128, 8, TB], BF16, n=8)
        hT = P.sb(ph, "hT", [128, NFC, TB], BF16, n=NFC)
        U = [P.sb(ph, "U%d" % i, [128, TB + 2], F32) for i in range(2)]
        acc = [P.sb(ph, "acc%d" % i, [128, TB], F32) for i in range(2)]
        G = P.sb(ph, "G", [128, TB], F32)
        Cc = P.sb(ph, "Cc", [128, NCH, 2], F32, n=NCH)
        cw = P.sb(ph, "cw", [128, 4, NCH], F32)
        g_bc = P.sb(ph, "g_bc", [128, D], F32)
        b_bc = P.sb(ph, "b_bc", [128, D], F32)
        ident = P.sb(ph, "ident", [128, 128], F32)
        stats = P.sb(ph, "stats", [128, 2, 6], F32)
        mv = P.sb(ph, "mv", [128, 2], F32)
        rstd = P.sb(ph, "rstd", [128, 1], F32)
        ptr = P.ps(ph, "ptr")
        pup = [P.ps(ph, "pup%d" % i) for i in range(2)]
        pdn = [P.ps(ph, "pdn%d" % i) for i in range(2)]

        load_weight_bf16(P, wup.t, wup.b, T.get("ffn_w_up", li), 0, D, 2 * DFF, 8)
        load_weight_bf16(P, wd.t, wd.b, T.get("ffn_w_down", li), 0, DFF, D, NFC)
        for j in range(3):
            src = AP(T.get("ffn_conv_w", li), j * 2 * DFF, [[1, 128], [128, NCH]])
            P.dma("sp", cw.t[:, j, :], src, writes=[cw.b], allow_slow_non_contiguous=True)
        src = AP(T.get("ffn_conv_b", li), 0, [[1, 128], [128, NCH]])
        P.dma("sp", cw.t[:, 3, :], src, writes=[cw.b], allow_slow_non_contiguous=True)
        P.dma("sp", g_bc.t[:, :], bcast_rows(T.get("ln_g", li), D, D), writes=[g_bc.b])
        P.dma("sp", b_bc.t[:, :], bcast_rows(T.get("ln_b", li), D, D), writes=[b_bc.b])
        P.dma("sp", ident.t[:, :], T.get("ident").ap(), writes=[ident.b])

        for b in range(NSEQ):
            for ch in range(NCH):
                P.op("pool", lambda e, ch=ch: e.memset(Cc.t[:, ch, :], 0.0), writes=[Cc.bs[ch]])
            for blk in range(S // TB):
                t0 = blk * TB
                for tt in range(NT):
                    src = AP(xin["t"], (b * S + t0 + tt * 128) * D, [[D, 128], [1, D]])
                    P.dma("sp", X.t[:, tt, :], src, reads=[xin["bufs"][b][(t0 // 128) + tt]], writes=[X.bs[tt]])
                for kc in range(8):
                    for tt in range(NT):
                        P.op("pe", lambda e, kc=kc, tt=tt: e.transpose(
                            ptr.t[:, tt * 128:(tt + 1) * 128], X.t[:, tt, kc * 128:(kc + 1) * 128], ident.t[:, :]),
                            reads=[X.bs[tt], ident.b], writes=[ptr.b])
                    P.op("act", lambda e, kc=kc: e.activation(out=xT.t[:, kc, :], in_=ptr.t[:, 0:TB], func=AF.Copy),
                         reads=[ptr.b], writes=[xT.bs[kc]])
                for fc in range(NFC):
                    for gi, ch in enumerate((fc, fc + NFC)):
                        pp = pup[gi]
                        for kc in range(8):
                            P.op("pe", lambda e, kc=kc, ch=ch, pp=pp: e.matmul(
                                pp.t[:, 0:TB], wup.t[:, kc, ch * 128:(ch + 1) * 128], xT.t[:, kc, :],
                                start=(kc == 0), stop=(kc == 7)),
                                reads=[wup.b, xT.bs[kc]], writes=[pp.b])
                        Ui = U[gi]
                        P.op("pool", lambda e, ch=ch, Ui=Ui: e.tensor_copy(out=Ui.t[:, 0:2], in_=Cc.t[:, ch, :]),
                             reads=[Cc.bs[ch]], writes=[Ui.b])
                        P.op("act", lambda e, pp=pp, Ui=Ui: e.activation(out=Ui.t[:, 2:TB + 2], in_=pp.t[:, 0:TB], func=AF.Copy),
                             reads=[pp.b], writes=[Ui.b])
                        P.op("pool", lambda e, ch=ch, Ui=Ui: e.tensor_copy(out=Cc.t[:, ch, :], in_=Ui.t[:, TB:TB + 2]),
                             reads=[Ui.b], writes=[Cc.bs[ch]])
                        a = acc[gi]
                        P.op("dve", lambda e, ch=ch, Ui=Ui, a=a: e.tensor_scalar(
                            out=a.t[:, :], in0=Ui.t[:, 2:TB + 2], scalar1=cw.t[:, 2, ch:ch + 1], scalar2=cw.t[:, 3, ch:ch + 1],
                            op0=ALU.mult, op1=ALU.add), reads=[Ui.b, cw.b], writes=[a.b])
                        P.op("dve", lambda e, ch=ch, Ui=Ui, a=a: e.scalar_tensor_tensor(
                            out=a.t[:, :], in0=Ui.t[:, 1:TB + 1], scalar=cw.t[:, 1, ch:ch + 1], in1=a.t[:, :],
                            op0=ALU.mult, op1=ALU.add), reads=[Ui.b, cw.b, a.b], writes=[a.b])
                        P.op("dve", lambda e, ch=ch, Ui=Ui, a=a: e.scalar_tensor_tensor(
                            out=a.t[:, :], in0=Ui.t[:, 0:TB], scalar=cw.t[:, 0, ch:ch + 1], in1=a.t[:, :],
                            op0=ALU.mult, op1=ALU.add), reads=[Ui.b, cw.b, a.b], writes=[a.b])
                    P.op("act", lambda e: e.activation(out=G.t[:, :], in_=acc[0].t[:, :], func=AF.Silu),
                         reads=[acc[0].b], writes=[G.b])
                    P.op("dve", lambda e, fc=fc: e.tensor_tensor(out=hT.t[:, fc, :], in0=G.t[:, :], in1=acc[1].t[:, :], op=ALU.mult),
                         reads=[G.b, acc[1].b], writes=[hT.bs[fc]])
                for tt in range(NT):
                    for hf in range(2):
                        pp = pdn[hf]
                        for fc in range(NFC):
                            P.op("pe", lambda e, fc=fc, tt=tt, hf=hf, pp=pp: e.matmul(
                                pp.t[:, :], hT.t[:, fc, tt * 128:(tt + 1) * 128], wd.t[:, fc, hf * 512:(hf + 1) * 512],
                                start=(fc == 0), stop=(fc == NFC - 1)),
                                reads=[hT.bs[fc], wd.b], writes=[pp.b])
                        P.op("dve", lambda e, tt=tt, hf=hf, pp=pp: e.scalar_tensor_tensor(
                            out=X.t[:, tt, hf * 512:(hf + 1) * 512], in0=X.t[:, tt, hf * 512:(hf + 1) * 512],
                            scalar=DN_ALPHA, in1=pp.t[:, :], op0=ALU.mult, op1=ALU.add),
                            reads=[X.bs[tt], pp.b], writes=[X.bs[tt]])
                    layer_norm_rows(P, X.t[:, tt, :], X.bs[tt], stats, mv, rstd, g_bc, b_bc)
                    dst = AP(xout["t"], (b * S + t0 + tt * 128) * D, [[D, 128], [1, D]])
                    P.dma("sp", dst, X.t[:, tt, :], reads=[X.bs[tt]], writes=[xout["bufs"][b][(t0 // 128) + tt]])
        P.barrier()


def sap(t, off, dims, parts=128, p0=0):
    ps = t[:].ap[0][0]
    return AP(t, p0 * ps + off, [[ps, parts]] + [list(d) for d in dims])


def outproj_ln_tile(P, Z, zb, Xr, xrb, wo, zT, ptr, pdn, ident, stats, mv, rstd, g_bc, b_bc):
    for kc in range(8):
        P.op("pe", lambda e, kc=kc: e.transpose(ptr.t[:, kc * 128:(kc + 1) * 128],
                                                 Z[:, kc * 128:(kc + 1) * 128], ident.t[:, :]),
             reads=[zb, ident.b], writes=[ptr.bs[kc // 4]])
        if kc % 4 == 3:
            g4 = kc // 4
            P.op("act", lambda e, g4=g4: e.activation(
                out=zT.t[:, g4 * 4:(g4 + 1) * 4, :],
                in_=ptr.t[:, g4 * 512:(g4 + 1) * 512].rearrange("p (a b) -> p a b", a=4), func=AF.Copy),
                reads=[ptr.bs[g4]], writes=[zT.b])
    for hf in range(2):
        pp = pdn[hf]
        for kc in range(8):
            P.op("pe", lambda e, kc=kc, hf=hf, pp=pp: e.matmul(
                pp.t[:, :], zT.t[:, kc, :], wo.t[:, kc, hf * 512:(hf + 1) * 512], start=(kc == 0), stop=(kc == 7)),
                reads=[zT.b, wo.b], writes=[pp.b])
        P.op("dve", lambda e, hf=hf, pp=pp: e.scalar_tensor_tensor(
            out=Xr[:, hf * 512:(hf + 1) * 512], in0=Xr[:, hf * 512:(hf + 1) * 512], scalar=DN_ALPHA, in1=pp.t[:, :],
            op0=ALU.mult, op1=ALU.add), reads=[xrb, pp.b], writes=[xrb])
    layer_norm_rows(P, Xr, xrb, stats, mv, rstd, g_bc, b_bc)


def tok_ap(stream, b, t0, S, n=128):
    return AP(stream["t"], (b * S + t0) * D, [[D, n], [1, D]])


NBR = {"r": 0, "k": 1, "v": 2, "w": 3, "a": 4, "g": 5}


def rwkv_pre_phase(P, cfg, li, j, xin, T, SC):
    nc = P.nc
    S, NSEQ = cfg.S, cfg.NSEQ
    TBR = 128
    NT = TBR // 128
    vres = j > 0
    with ExitStack() as ph:
        wrkv = P.sb(ph, "wrkv", [128, 24, D], BF16)
        l1 = P.sb(ph, "l1", [128, 8, 320], BF16)
        l2 = P.sb(ph, "l2", [128, 5, D], BF16)
        bc = P.sb(ph, "bc", [128, 6, D], F32)
        mixT = P.sb(ph, "mixT", [128, 6, 8], F32)
        ident = P.sb(ph, "ident", [128, 128], F32)
        Xs = [P.sb(ph, "Xs%d" % i, [128, D], F32) for i in range(2)]
        xTf = P.sb(ph, "xTf", [128, 8, TBR + 1], F32)
        xxT = P.sb(ph, "xxT", [128, 8, TBR], F32)
        tmpm = [P.sb(ph, "tmpm%d" % i, [128, 8, TBR], F32) for i in range(2)]
        xm = [P.sb(ph, "xm%d" % i, [128, 8, TBR], BF16) for i in range(2)]
        hl = {k: P.sb(ph, "hl_" + k, [128, TBR], BF16) for k in ("w", "a", "g0", "g1", "v")}
        names = ["R", "K", "V", "ZW", "ZA", "G", "ZV", "KK", "TMP", "BT", "VF"]
        EP = P.sb(ph, "EP", [128, D], F32)
        EM = P.sb(ph, "EM", [128, D], F32)
        FT = [P.sb(ph, "FT%d" % i, [128, 4, 128], F32) for i in range(2)]
        tri2 = P.sb(ph, "tri2", [128, 128], F32)
        nft = 0
        tl = {k: [P.sb(ph, "%s%d" % (k, tt), [128, D], F32) for tt in range(NT)] for k in names}
        ss = P.sb(ph, "ss", [128, 16], F32)
        bs = P.sb(ph, "bs", [128, 16], F32)
        ptr = P.ps(ph, "ptr")
        pp2 = [[P.ps(ph, "pp%d%d" % (i, h)) for h in range(2)] for i in range(2)]
        pl = [P.ps(ph, "pl%d" % i) for i in range(2)]

        for br in range(3):
            load_weight_bf16(P, wrkv.t[:, br * 8:(br + 1) * 8, :], wrkv.b, T.get("rwkv_w_rkv", j), br * D * D, D, D, 8)
        for (nm, c0, w) in (("rwkv_w1", 0, 64), ("rwkv_a1", 64, 64), ("rwkv_g1", 128, 160)) + ((("rwkv_v1", 288, 32),) if vres else ()):
            jj = j - 1 if nm == "rwkv_v1" else j
            for kc in range(8):
                src = AP(T.get(nm, jj), kc * 128 * w, [[w, 128], [1, w]])
                P.dma("pool", l1.t[:, kc, c0:c0 + w], src, writes=[l1.b])
        for (nm, slot, r0, rows) in (("rwkv_w2", 0, 0, 64), ("rwkv_a2", 1, 0, 64), ("rwkv_g2", 2, 0, 128), ("rwkv_g2", 3, 128, 32)) + ((("rwkv_v2", 4, 0, 32),) if vres else ()):
            jj = j - 1 if nm == "rwkv_v2" else j
            src = AP(T.get(nm, jj), r0 * D, [[D, rows], [1, D]])
            P.dma("pool", l2.t[0:rows, slot, :], src, writes=[l2.b])
        for i, nm in enumerate(("rwkv_w0", "rwkv_a0", "rwkv_v0", "rwkv_k_k", "rwkv_k_a", "rwkv_r_k")):
            if nm == "rwkv_v0":
                if not vres:
                    continue
                P.dma("sp", bc.t[:, i, :], bcast_rows(T.get(nm, j - 1), 0, D), writes=[bc.b])
            else:
                P.dma("sp", bc.t[:, i, :], bcast_rows(T.get(nm, j), 0, D), writes=[bc.b])
        for i in range(6):
            src = AP(T.get("rwkv_mix", j), i * D, [[1, 128], [128, 8]])
            P.dma("sp", mixT.t[:, i, :], src, writes=[mixT.b], allow_slow_non_contiguous=True)
        P.dma("sp", ident.t[:, :], T.get("ident").ap(), writes=[ident.b])
        P.dma("sp", tri2.t[:, :], T.get("tri2").ap(), writes=[tri2.b])

        nblk = 0
        for b in range(NSEQ):
            for blk in range(S // TBR):
                t0 = blk * TBR
                if blk == 0:
                    P.op("pool", lambda e: e.memset(xTf.t[:, :, 0:1], 0.0), writes=[xTf.b])
                else:
                    P.op("pool", lambda e: e.tensor_copy(out=xTf.t[:, :, 0:1], in_=xTf.t[:, :, TBR:TBR + 1]),
                         reads=[xTf.b], writes=[xTf.b])
                for tt in range(NT):
                    Xt = Xs[tt % 2]
                    P.dma("sp", Xt.t[:, :], tok_ap(xin, b, t0 + tt * 128, S),
                          reads=[xin["bufs"][b][t0 // 128 + tt]], writes=[Xt.b])
                    for g4 in range(2):
                        for q in range(4):
                            kc = g4 * 4 + q
                            P.op("pe", lambda e, kc=kc, q=q, Xt=Xt: e.transpose(
                                ptr.t[:, q * 128:(q + 1) * 128], Xt.t[:, kc * 128:(kc + 1) * 128], ident.t[:, :]),
                                reads=[Xt.b, ident.b], writes=[ptr.b])
                        P.op("act", lambda e, g4=g4, tt=tt: e.activation(
                            out=xTf.t[:, g4 * 4:(g4 + 1) * 4, 1 + tt * 128:1 + (tt + 1) * 128],
                            in_=ptr.t[:, :].rearrange("p (a b) -> p a b", a=4), func=AF.Copy),
                            reads=[ptr.b], writes=[xTf.b])
                P.op("dve", lambda e: e.tensor_tensor(out=xxT.t[:, :, :], in0=xTf.t[:, :, 0:TBR], in1=xTf.t[:, :, 1:TBR + 1],
                                                      op=ALU.subtract), reads=[xTf.b], writes=[xxT.b])

                def make_xm(i, slot):
                    tm, xo = tmpm[slot], xm[slot]
                    P.op("pool", lambda e: e.tensor_tensor(
                        out=tm.t[:, :, :], in0=xxT.t[:, :, :], in1=mixT.t[:, i, :].unsqueeze(2).to_broadcast([128, 8, TBR]),
                        op=ALU.mult), reads=[xxT.b, mixT.b], writes=[tm.b])
                    P.op("dve", lambda e: e.tensor_tensor(out=xo.t[:, :, :], in0=tm.t[:, :, :], in1=xTf.t[:, :, 1:TBR + 1],
                                                          op=ALU.add), reads=[tm.b, xTf.b], writes=[xo.b])
                    return xo

                def proj_full(xo, br, dst_key):
                    for tt in range(NT):
                        pr = pp2[tt % 2]
                        for hf in range(2):
                            for kc in range(8):
                                P.op("pe", lambda e, kc=kc, hf=hf, tt=tt, pr=pr: e.matmul(
                                    pr[hf].t[:, :], xo.t[:, kc, tt * 128:(tt + 1) * 128],
                                    wrkv.t[:, br * 8 + kc, hf * 512:(hf + 1) * 512], start=(kc == 0), stop=(kc == 7)),
                                    reads=[xo.b, wrkv.b], writes=[pr[hf].b])
                            dst = tl[dst_key][tt]
                            P.op("act", lambda e, hf=hf, pr=pr, dst=dst: e.activation(
                                out=dst.t[:, hf * 512:(hf + 1) * 512], in_=pr[hf].t[:, :], func=AF.Copy),
                                reads=[pr[hf].b], writes=[dst.b])

                def lora1(xo, c0, w, key, func, pi):
                    pq = pl[pi]
                    for kc in range(8):
                        P.op("pe", lambda e, kc=kc: e.matmul(pq.t[0:w, 0:TBR], l1.t[:, kc, c0:c0 + w], xo.t[:, kc, :],
                                                             start=(kc == 0), stop=(kc == 7)),
                             reads=[xo.b, l1.b], writes=[pq.b])
                    P.op("act", lambda e: e.activation(out=hl[key].t[0:w, :], in_=pq.t[0:w, 0:TBR], func=func),
                         reads=[pq.b], writes=[hl[key].b])

                def lora2(parts, dst_key):
                    for tt in range(NT):
                        pr = pp2[tt % 2]
                        for hf in range(2):
                            for n, (key, rows, slot) in enumerate(parts):
                                P.op("pe", lambda e, key=key, rows=rows, slot=slot, n=n, hf=hf, tt=tt, pr=pr: e.matmul(
                                    pr[hf].t[:, :], hl[key].t[0:rows, tt * 128:(tt + 1) * 128],
                                    l2.t[0:rows, slot, hf * 512:(hf + 1) * 512], start=(n == 0), stop=(n == len(parts) - 1)),
                                    reads=[hl[key].b, l2.b], writes=[pr[hf].b])
                            dst = tl[dst_key][tt]
                            P.op("act", lambda e, hf=hf, pr=pr, dst=dst: e.activation(
                                out=dst.t[:, hf * 512:(hf + 1) * 512], in_=pr[hf].t[:, :], func=AF.Copy),
                                reads=[pr[hf].b], writes=[dst.b])

                xo = make_xm(0, 0); proj_full(xo, 0, "R")
                xo = make_xm(1, 1); proj_full(xo, 1, "K")
                xo = make_xm(2, 0); proj_full(xo, 2, "V")
                if vres:
                    lora1(xo, 288, 32, "v", AF.Copy, 0)
                    lora2([("v", 32, 4)], "ZV")
                xo = make_xm(3, 1); lora1(xo, 0, 64, "w", AF.Tanh, 1); lora2([("w", 64, 0)], "ZW")
                xo = make_xm(4, 0); lora1(xo, 64, 64, "a", AF.Copy, 0); lora2([("a", 64, 1)], "ZA")
                xo = make_xm(5, 1)
                lora1(xo, 128, 128, "g0", AF.Sigmoid, 1)
                lora1(xo, 256, 32, "g1", AF.Sigmoid, 0)
                lora2([("g0", 128, 2), ("g1", 32, 3)], "G")

                for tt in range(NT):
                    R, Kt, V, ZW, ZA, G, ZV, KK, TMP, BT, VF = [tl[k][tt] for k in names]
                    tb = xin["bufs"]
                    tix = t0 // 128 + tt
                    v3 = lambda t: t.t[:, :].rearrange("p (h n) -> p h n", h=16)
                    P.op("dve", lambda e: e.tensor_tensor(out=ZW.t[:, :], in0=ZW.t[:, :], in1=bc.t[:, 0, :], op=ALU.add),
                         reads=[ZW.b, bc.b], writes=[ZW.b])
                    P.op("act", lambda e: e.activation(out=ZW.t[:, :], in_=ZW.t[:, :], func=AF.Sigmoid), reads=[ZW.b], writes=[ZW.b])
                    P.op("dve", lambda e: e.tensor_tensor(out=ZA.t[:, :], in0=ZA.t[:, :], in1=bc.t[:, 1, :], op=ALU.add),
                         reads=[ZA.b, bc.b], writes=[ZA.b])
                    P.op("act", lambda e: e.activation(out=ZA.t[:, :], in_=ZA.t[:, :], func=AF.Sigmoid), reads=[ZA.b], writes=[ZA.b])
                    P.op("dve", lambda e: e.tensor_tensor(out=KK.t[:, :], in0=Kt.t[:, :], in1=bc.t[:, 3, :], op=ALU.mult),
                         reads=[Kt.b, bc.b], writes=[KK.b])
                    P.op("act", lambda e: e.activation(out=TMP.t[:, :], in_=KK.t[:, :], func=AF.Square), reads=[KK.b], writes=[TMP.b])
                    P.op("dve", lambda e: e.tensor_reduce(out=ss.t[:, :], in_=v3(TMP), axis=AX.X, op=ALU.add),
                         reads=[TMP.b], writes=[ss.b])
                    P.op("act", lambda e: e.activation(out=ss.t[:, :], in_=ss.t[:, :], func=AF.Sqrt), reads=[ss.b], writes=[ss.b])
                    P.op("dve", lambda e: e.tensor_scalar_max(out=ss.t[:, :], in0=ss.t[:, :], scalar1=1e-12), reads=[ss.b], writes=[ss.b])
                    P.op("dve", lambda e: e.reciprocal(out=ss.t[:, :], in_=ss.t[:, :]), reads=[ss.b], writes=[ss.b])
                    P.op("dve", lambda e: e.tensor_tensor(out=v3(KK), in0=v3(KK), in1=ss.t[:, :].unsqueeze(2).to_broadcast([128, 16, 64]),
                                                          op=ALU.mult), reads=[KK.b, ss.b], writes=[KK.b])
                    P.op("pool", lambda e: e.tensor_tensor(out=BT.t[:, :], in0=KK.t[:, :], in1=ZA.t[:, :], op=ALU.mult),
                         reads=[KK.b, ZA.b], writes=[BT.b])
                    P.op("dve", lambda e: e.scalar_tensor_tensor(out=TMP.t[:, :], in0=ZA.t[:, :], scalar=-1.0, in1=bc.t[:, 4, :],
                                                                 op0=ALU.add, op1=ALU.mult), reads=[ZA.b, bc.b, TMP.b], writes=[TMP.b])
                    P.op("dve", lambda e: e.scalar_tensor_tensor(out=Kt.t[:, :], in0=TMP.t[:, :], scalar=1.0, in1=Kt.t[:, :],
                                                                 op0=ALU.add, op1=ALU.mult), reads=[TMP.b, Kt.b], writes=[Kt.b])
                    P.op("act", lambda e: e.mul(out=KK.t[:, :], in_=KK.t[:, :], mul=-1.0) if False else
                         e.activation(out=KK.t[:, :], in_=KK.t[:, :], func=AF.Copy, scale=-1.0), reads=[KK.b, BT.b], writes=[KK.b])
                    if vres:
                        P.dma("sp", VF.t[:, :], tok_ap(SC["vfirst"], b, t0 + tt * 128, S),
                              reads=[SC["vfirst"]["bufs"][b][tix]], writes=[VF.b])
                        P.op("dve", lambda e: e.tensor_tensor(out=ZV.t[:, :], in0=ZV.t[:, :], in1=bc.t[:, 2, :], op=ALU.add),
                             reads=[ZV.b, bc.b], writes=[ZV.b])
                        P.op("act", lambda e: e.activation(out=ZV.t[:, :], in_=ZV.t[:, :], func=AF.Sigmoid), reads=[ZV.b], writes=[ZV.b])
                        P.op("pool", lambda e: e.tensor_tensor(out=VF.t[:, :], in0=VF.t[:, :], in1=V.t[:, :], op=ALU.subtract),
                             reads=[VF.b, V.b], writes=[VF.b])
                        P.op("pool", lambda e: e.tensor_tensor(out=VF.t[:, :], in0=VF.t[:, :], in1=ZV.t[:, :], op=ALU.mult),
                             reads=[VF.b, ZV.b], writes=[VF.b])
                        P.op("dve", lambda e: e.tensor_tensor(out=V.t[:, :], in0=V.t[:, :], in1=VF.t[:, :], op=ALU.add),
                             reads=[V.b, VF.b], writes=[V.b])
                    else:
                        P.dma("sp", tok_ap(SC["vfirst"], b, t0 + tt * 128, S), V.t[:, :], reads=[V.b],
                              writes=[SC["vfirst"]["bufs"][b][tix]])
                    P.dma("sp", tok_ap(SC["v"], b, t0 + tt * 128, S), V.t[:, :], reads=[V.b], writes=[SC["v"]["bufs"][b][tix]])
                    P.dma("sp", tok_ap(SC["g"], b, t0 + tt * 128, S), G.t[:, :], reads=[G.b], writes=[SC["g"]["bufs"][b][tix]])
                    P.op("dve", lambda e: e.tensor_tensor(out=TMP.t[:, :], in0=R.t[:, :], in1=Kt.t[:, :], op=ALU.mult),
                         reads=[R.b, Kt.b, TMP.b], writes=[TMP.b])
                    P.op("pool", lambda e: e.tensor_tensor(out=TMP.t[:, :], in0=TMP.t[:, :], in1=bc.t[:, 5, :], op=ALU.mult),
                         reads=[TMP.b, bc.b], writes=[TMP.b])
                    P.op("dve", lambda e: e.tensor_reduce(out=bs.t[:, :], in_=v3(TMP), axis=AX.X, op=ALU.add),
                         reads=[TMP.b], writes=[bs.b])
                    P.op("dve", lambda e: e.tensor_tensor(out=v3(TMP), in0=v3(V), in1=bs.t[:, :].unsqueeze(2).to_broadcast([128, 16, 64]),
                                                          op=ALU.mult), reads=[V.b, bs.b, TMP.b], writes=[TMP.b])
                    P.dma("sp", tok_ap(SC["bonus"], b, t0 + tt * 128, S), TMP.t[:, :], reads=[TMP.b], writes=[SC["bonus"]["bufs"][b][tix]])
                    CD = math.exp(-0.5)
                    pcs = pp2[tt % 2]
                    for hf in range(2):
                        P.op("pe", lambda e, hf=hf: e.matmul(pcs[hf].t[:, :], tri2.t[:, :], ZW.t[:, hf * 512:(hf + 1) * 512], start=True, stop=True),
                             reads=[tri2.b, ZW.b], writes=[pcs[hf].b])
                        P.op("act", lambda e, hf=hf: e.activation(out=EP.t[:, hf * 512:(hf + 1) * 512], in_=pcs[hf].t[:, :], func=AF.Exp, scale=-CD),
                             reads=[pcs[hf].b, EP.b], writes=[EP.b])
                        P.op("act", lambda e, hf=hf: e.activation(out=EM.t[:, hf * 512:(hf + 1) * 512], in_=pcs[hf].t[:, :], func=AF.Exp, scale=CD),
                             reads=[pcs[hf].b, EM.b], writes=[EM.b])
                    P.op("act", lambda e: e.activation(out=ZW.t[:, :], in_=ZW.t[:, :], func=AF.Exp, scale=CD), reads=[ZW.b], writes=[ZW.b])
                    P.op("dve", lambda e: e.tensor_tensor(out=ZW.t[:, :], in0=ZW.t[:, :], in1=EP.t[:, :], op=ALU.mult), reads=[ZW.b, EP.b], writes=[ZW.b])
                    P.op("dve", lambda e: e.tensor_tensor(out=R.t[:, :], in0=R.t[:, :], in1=EP.t[:, :], op=ALU.mult), reads=[R.b, EP.b], writes=[R.b])
                    P.op("pool", lambda e: e.tensor_tensor(out=Kt.t[:, :], in0=Kt.t[:, :], in1=EM.t[:, :], op=ALU.mult), reads=[Kt.b, EM.b], writes=[Kt.b])
                    P.op("dve", lambda e: e.tensor_tensor(out=BT.t[:, :], in0=BT.t[:, :], in1=EM.t[:, :], op=ALU.mult), reads=[BT.b, EM.b], writes=[BT.b])
                    P.op("pool", lambda e: e.tensor_tensor(out=KK.t[:, :], in0=KK.t[:, :], in1=ZW.t[:, :], op=ALU.mult), reads=[KK.b, ZW.b], writes=[KK.b])
                    P.dma("sp", tok_ap(SC["k"], b, t0 + tt * 128, S), Kt.t[:, :], reads=[Kt.b], writes=[SC["k"]["bufs"][b][tix]])
                    P.dma("sp", tok_ap(SC["b"], b, t0 + tt * 128, S), BT.t[:, :], reads=[BT.b], writes=[SC["b"]["bufs"][b][tix]])
                    for cc in range(2):
                        dst = AP(SC["wc"]["t"], ((b * (S // 64)) + (t0 + tt * 128) // 64 + cc) * D, [[D, 1], [1, D]])
                        P.dma("sp", dst, EP.t[cc * 64 + 63:cc * 64 + 64, :], reads=[EP.b], writes=[SC["wc"]["bufs"][b][tix]])
                    for key, src_t in (("fa", KK), ("fb", BT), ("fk", Kt), ("fr", R)):
                        for g4 in range(2):
                            pq = pcs[g4]
                            for q in range(4):
                                hp = g4 * 4 + q
                                P.op("pe", lambda e, q=q, hp=hp, pq=pq, src_t=src_t: e.transpose(
                                    pq.t[:, q * 128:(q + 1) * 128], src_t.t[:, hp * 128:(hp + 1) * 128], ident.t[:, :]),
                                    reads=[src_t.b, ident.b], writes=[pq.b])
                            ft = FT[nft % 2]
                            nft += 1
                            P.op("act", lambda e, pq=pq, ft=ft: e.activation(out=ft.t[:, :, :], in_=pq.t[:, :].rearrange("p (a b) -> p a b", a=4), func=AF.Copy),
                                 reads=[pq.b, ft.b], writes=[ft.b])
                            dst = AP(SC[key]["t"], (b * 1024 + g4 * 512) * S + t0 + tt * 128, [[S, 128], [128 * S, 4], [1, 128]])
                            P.dma("sp", dst, ft.t[:, :, :], reads=[ft.b], writes=[SC[key]["bufs"][b][tix]])
        P.barrier()


def rwkv_scan_phase(P, cfg, SC, T):
    S, NSEQ = cfg.S, cfg.NSEQ
    C = 64
    NCH_ = S // C
    with ExitStack() as ph:
        FM = [P.sb(ph, "FM%d" % i, [64, 4, 16, 64], F32) for i in range(2)]
        TM = [P.sb(ph, "TM%d" % i, [64, 3, D], F32) for i in range(2)]
        WcT = [P.sb(ph, "WcT%d" % i, [64, 16], F32) for i in range(2)]
        S0T = P.sb(ph, "S0T", [64, 16, 64], F32, n=2)
        mk = P.sb(ph, "mk", [64, 3, 64], F32)
        idn = P.sb(ph, "idn", [128, 128], F32)
        A_ = [[P.sb(ph, "A%d%d" % (g, i), [64, 8, 64], F32) for i in range(2)] for g in range(2)]
        B_ = [[P.sb(ph, "B%d%d" % (g, i), [64, 8, 64], F32) for i in range(2)] for g in range(2)]
        X_ = [[P.sb(ph, "X%d%d" % (g, i), [64, 8, 64], F32) for i in range(2)] for g in range(2)]
        Mm = [P.sb(ph, "Mm%d" % g, [64, 8, 64], F32) for g in range(2)]
        Pb = [P.sb(ph, "Pb%d" % g, [64, 8, 64], F32) for g in range(2)]
        Pk = [P.sb(ph, "Pk%d" % g, [64, 8, 64], F32) for g in range(2)]
        Y = [P.sb(ph, "Ych%d" % i, [64, 16, 64], F32, n=2) for i in range(2)]
        pz = [[P.ps(ph, "pz%d%d" % (g, i)) for i in range(4)] for g in range(2)]
        P.dma("sp", mk.t[:, :, :], T.get("scan_masks").ap(), writes=[mk.b])
        P.dma("sp", idn.t[:, :], T.get("ident").ap(), writes=[idn.b])

        def load(b, c, slot):
            tix = (c * C) // 128
            for ti, key in enumerate(("fa", "fb", "fk", "fr")):
                src = AP(SC[key]["t"], b * 1024 * S + c * C, [[S, 64], [64 * S, 16], [1, 64]])
                P.dma("sp", FM[slot].t[:, ti, :, :], src, reads=[SC[key]["bufs"][b][tix]], writes=[FM[slot].b])
            for ti, key in enumerate(("v", "b", "k")):
                src = AP(SC[key]["t"], (b * S + c * C) * D, [[D, 64], [1, D]])
                P.dma("sp", TM[slot].t[:, ti, :], src, reads=[SC[key]["bufs"][b][tix]], writes=[TM[slot].b])
            src = AP(SC["wc"]["t"], (b * NCH_ + c) * D, [[1, 64], [64, 16]])
            P.dma("sp", WcT[slot].t[:, :], src, reads=[SC["wc"]["bufs"][b][tix]], writes=[WcT[slot].b], allow_slow_non_contiguous=True)

        def mm8(g, ps, lhs_fn, rhs_fn, start, stop, reads):
            for p in range(8):
                P.op("pe", lambda e, p=p: e.matmul(ps.t[0:64, p * 64:(p + 1) * 64], lhs_fn(p), rhs_fn(p), start=start, stop=stop),
                     reads=reads, writes=[ps.b])

        def evac_mask(g, dst, ps, mi):
            P.op("dve", lambda e: e.tensor_tensor(
                out=dst.t[:, :, :], in0=ps.t[0:64, :].rearrange("p (a b) -> p a b", a=8),
                in1=sap(mk.t, mi * 64, [[0, 8], [1, 64]], parts=64), op=ALU.mult), reads=[ps.b, mk.b, dst.b], writes=[dst.b])

        def evac_copy(g, dst_ap, dstb, ps):
            P.op("act", lambda e: e.activation(out=dst_ap, in_=ps.t[0:64, :].rearrange("p (a b) -> p a b", a=8), func=AF.Copy),
                 reads=[ps.b, dstb], writes=[dstb])

        order = [(b, c) for b in range(NSEQ) for c in range(NCH_)]
        load(order[0][0], order[0][1], 0)
        for n, (b, c) in enumerate(order):
            slot = n % 2
            if n + 1 < len(order):
                load(order[n + 1][0], order[n + 1][1], 1 - slot)
            fm, tm, wc = FM[slot], TM[slot], WcT[slot]
            if c == 0:
                for g in range(2):
                    P.op("pool", lambda e, g=g: e.memset(S0T.t[:, g * 8:(g + 1) * 8, :], 0.0), writes=[S0T.bs[g]])
            yb = Y[n % 2]
            f = lambda ti: (lambda g: (lambda p: fm.t[:, ti, g * 8 + p, :]))
            fa, fb, fk, fr = f(0), f(1), f(2), f(3)
            tmv = lambda ti: (lambda g: (lambda p: tm.t[:, ti, (g * 8 + p) * 64:(g * 8 + p + 1) * 64]))
            tv, tb, tk = tmv(0), tmv(1), tmv(2)
            s0 = lambda g: (lambda p: S0T.t[:, g * 8 + p, :])
            G = (0, 1)
            for g in G:
                mm8(g, pz[g][0], fb(g), fa(g), True, True, [fm.b])
                mm8(g, pz[g][1], fa(g), fb(g), True, True, [fm.b])
            for g in G:
                evac_mask(g, A_[g][0], pz[g][0], 0)
                evac_mask(g, B_[g][0], pz[g][1], 1)
            for g in G:
                mm8(g, pz[g][2], fk(g), fa(g), True, True, [fm.b])
                mm8(g, pz[g][3], fb(g), fr(g), True, True, [fm.b])
                mm8(g, pz[g][0], fk(g), fr(g), True, True, [fm.b])
            for g in G:
                evac_mask(g, Mm[g], pz[g][2], 0)
                evac_mask(g, Pb[g], pz[g][3], 2)
                evac_mask(g, Pk[g], pz[g][0], 2)
            for g in G:
                ps = pz[g][1]
                for p in range(8):
                    P.op("pe", lambda e, p=p, g=g, ps=ps: e.matmul(ps.t[0:64, p * 64:(p + 1) * 64], fa(g)(p), s0(g)(p), start=True, stop=False),
                         reads=[fm.b, S0T.bs[g]], writes=[ps.b])
                    P.op("pe", lambda e, p=p, g=g, ps=ps: e.matmul(ps.t[0:64, p * 64:(p + 1) * 64], Mm[g].t[:, p, :], tv(g)(p), start=False, stop=True),
                         reads=[Mm[g].b, tm.b], writes=[ps.b])
            for g in G:
                evac_copy(g, X_[g][0].t[:, :, :], X_[g][0].b, pz[g][1])
            for i in range(6):
                cur, nxt = i % 2, (i + 1) % 2
                for g in G:
                    mm8(g, pz[g][2], (lambda p, g=g: A_[g][cur].t[:, p, :]), (lambda p, g=g: X_[g][cur].t[:, p, :]), True, True,
                        [A_[g][cur].b, X_[g][cur].b])
                    if i < 5:
                        mm8(g, pz[g][0], (lambda p, g=g: B_[g][cur].t[:, p, :]), (lambda p, g=g: A_[g][cur].t[:, p, :]), True, True,
                            [A_[g][cur].b, B_[g][cur].b])
                        if i < 4:
                            mm8(g, pz[g][1], (lambda p, g=g: A_[g][cur].t[:, p, :]), (lambda p, g=g: B_[g][cur].t[:, p, :]), True, True,
                                [A_[g][cur].b, B_[g][cur].b])
                for g in G:
                    P.op("dve", lambda e, g=g: e.tensor_tensor(
                        out=X_[g][nxt].t[:, :, :], in0=pz[g][2].t[0:64, :].rearrange("p (a b) -> p a b", a=8), in1=X_[g][cur].t[:, :, :], op=ALU.add),
                        reads=[pz[g][2].b, X_[g][cur].b, X_[g][nxt].b], writes=[X_[g][nxt].b])
                    if i < 5:
                        evac_copy(g, A_[g][nxt].t[:, :, :], A_[g][nxt].b, pz[g][0])
                        if i < 4:
                            evac_copy(g, B_[g][nxt].t[:, :, :], B_[g][nxt].b, pz[g][1])
            UT = [X_[g][0] for g in G]
            for g in G:
                ps = pz[g][3]
                for p in range(8):
                    P.op("pe", lambda e, p=p, g=g, ps=ps: e.matmul(ps.t[0:64, p * 64:(p + 1) * 64], fr(g)(p), s0(g)(p), start=True, stop=False),
                         reads=[fm.b, S0T.bs[g]], writes=[ps.b])
                    P.op("pe", lambda e, p=p, g=g, ps=ps: e.matmul(ps.t[0:64, p * 64:(p + 1) * 64], Pb[g].t[:, p, :], UT[g].t[:, p, :], start=False, stop=False),
                         reads=[Pb[g].b, UT[g].b], writes=[ps.b])
                    P.op("pe", lambda e, p=p, g=g, ps=ps: e.matmul(ps.t[0:64, p * 64:(p + 1) * 64], Pk[g].t[:, p, :], tv(g)(p), start=False, stop=True),
                         reads=[Pk[g].b, tm.b], writes=[ps.b])
            for g in G:
                evac_copy(g, yb.t[:, g * 8:(g + 1) * 8, :], yb.bs[g], pz[g][3])
            for g in G:
                ps = pz[g][2]
                for p in range(8):
                    P.op("pe", lambda e, p=p, g=g, ps=ps: e.matmul(ps.t[0:64, p * 64:(p + 1) * 64], idn.t[0:64, 0:64], s0(g)(p), start=True, stop=False),
                         reads=[idn.b, S0T.bs[g]], writes=[ps.b])
                    P.op("pe", lambda e, p=p, g=g, ps=ps: e.matmul(ps.t[0:64, p * 64:(p + 1) * 64], tb(g)(p), UT[g].t[:, p, :], start=False, stop=False),
                         reads=[tm.b, UT[g].b], writes=[ps.b])
                    P.op("pe", lambda e, p=p, g=g, ps=ps: e.matmul(ps.t[0:64, p * 64:(p + 1) * 64], tk(g)(p), tv(g)(p), start=False, stop=True),
                         reads=[tm.b], writes=[ps.b])
            for g in G:
                P.op("dve", lambda e, g=g: e.tensor_tensor(
                    out=S0T.t[:, g * 8:(g + 1) * 8, :], in0=pz[g][2].t[0:64, :].rearrange("p (a b) -> p a b", a=8),
                    in1=wc.t[:, g * 8:(g + 1) * 8].unsqueeze(2).to_broadcast([64, 8, 64]), op=ALU.mult),
                    reads=[pz[g][2].b, wc.b, S0T.bs[g]], writes=[S0T.bs[g]])
            dst = AP(SC["y"]["t"], (b * S + c * C) * D, [[D, 64], [1, D]])
            P.dma("sp", dst, yb.t[:, :, :], reads=[yb.bs[0], yb.bs[1]], writes=[SC["y"]["bufs"][b][(c * C) // 128]])
        P.barrier()


def rwkv_post_phase(P, cfg, li, j, xin, xout, T, SC):
    S, NSEQ = cfg.S, cfg.NSEQ
    with ExitStack() as ph:
        wo = P.sb(ph, "wo", [128, 8, D], BF16)
        bc = P.sb(ph, "bc", [128, 2, D], F32)
        g_bc = P.sb(ph, "g_bc", [128, D], F32)
        b_bc = P.sb(ph, "b_bc", [128, D], F32)
        ident = P.sb(ph, "ident", [128, 128], F32)
        Y = [P.sb(ph, "Y%d" % i, [128, D], F32) for i in range(2)]
        BO = [P.sb(ph, "BO%d" % i, [128, D], F32) for i in range(2)]
        Gt = [P.sb(ph, "Gt%d" % i, [128, D], F32) for i in range(2)]
        Xr = [P.sb(ph, "Xr%d" % i, [128, D], F32) for i in range(2)]
        SQ = P.sb(ph, "SQ", [128, D], F32)
        zT = P.sb(ph, "zT", [128, 8, 128], BF16)
        s1 = P.sb(ph, "s1", [128, 16], F32)
        s2 = P.sb(ph, "s2", [128, 16], F32)
        m2 = P.sb(ph, "m2", [128, 16], F32)
        stats = P.sb(ph, "stats", [128, 2, 6], F32)
        mv = P.sb(ph, "mv", [128, 2], F32)
        rstd = P.sb(ph, "rstd", [128, 1], F32)
        ptr = P.ps(ph, "ptr", shape=(128, 1024), n=2)
        pdn = [P.ps(ph, "pdn%d" % i) for i in range(2)]
        load_weight_bf16(P, wo.t, wo.b, T.get("rwkv_w_o", j), 0, D, D, 8)
        P.dma("sp", bc.t[:, 0, :], bcast_rows(T.get("rwkv_lnx_g", j), 0, D), writes=[bc.b])
        P.dma("sp", bc.t[:, 1, :], bcast_rows(T.get("rwkv_lnx_b", j), 0, D), writes=[bc.b])
        P.dma("sp", g_bc.t[:, :], bcast_rows(T.get("ln_g", li), 0, D), writes=[g_bc.b])
        P.dma("sp", b_bc.t[:, :], bcast_rows(T.get("ln_b", li), 0, D), writes=[b_bc.b])
        P.dma("sp", ident.t[:, :], T.get("ident").ap(), writes=[ident.b])
        n = 0
        for b in range(NSEQ):
            for tix in range(S // 128):
                y, bo, gt, xr = Y[n % 2], BO[n % 2], Gt[n % 2], Xr[n % 2]
                n += 1
                t0 = tix * 128
                P.dma("sp", y.t[:, :], tok_ap(SC["y"], b, t0, S), reads=[SC["y"]["bufs"][b][tix]], writes=[y.b])
                P.dma("sp", bo.t[:, :], tok_ap(SC["bonus"], b, t0, S), reads=[SC["bonus"]["bufs"][b][tix]], writes=[bo.b])
                P.dma("sp", gt.t[:, :], tok_ap(SC["g"], b, t0, S), reads=[SC["g"]["bufs"][b][tix]], writes=[gt.b])
                P.dma("sp", xr.t[:, :], tok_ap(xin, b, t0, S), reads=[xin["bufs"][b][tix]], writes=[xr.b])
                y3 = y.t[:, :].rearrange("p (h n) -> p h n", h=16)
                P.op("dve", lambda e: e.tensor_reduce(out=s1.t[:, :], in_=y3, axis=AX.X, op=ALU.add), reads=[y.b], writes=[s1.b])
                P.op("act", lambda e: e.activation(out=SQ.t[:, :], in_=y.t[:, :], func=AF.Square), reads=[y.b], writes=[SQ.b])
                P.op("dve", lambda e: e.tensor_reduce(out=s2.t[:, :], in_=SQ.t[:, :].rearrange("p (h n) -> p h n", h=16),
                                                      axis=AX.X, op=ALU.add), reads=[SQ.b], writes=[s2.b])
                P.op("dve", lambda e: e.tensor_scalar(out=s1.t[:, :], in0=s1.t[:, :], scalar1=1.0 / 64, scalar2=None, op0=ALU.mult),
                     reads=[s1.b], writes=[s1.b])
                P.op("dve", lambda e: e.tensor_tensor(out=m2.t[:, :], in0=s1.t[:, :], in1=s1.t[:, :], op=ALU.mult),
                     reads=[s1.b], writes=[m2.b])
                P.op("dve", lambda e: e.scalar_tensor_tensor(out=s2.t[:, :], in0=s2.t[:, :], scalar=1.0 / 64, in1=m2.t[:, :],
                                                             op0=ALU.mult, op1=ALU.subtract), reads=[s2.b, m2.b], writes=[s2.b])
                P.op("act", lambda e: e.activation(out=s2.t[:, :], in_=s2.t[:, :], func=AF.Sqrt, bias=64e-5, scale=1.0),
                     reads=[s2.b], writes=[s2.b])
                P.op("dve", lambda e: e.reciprocal(out=s2.t[:, :], in_=s2.t[:, :]), reads=[s2.b], writes=[s2.b])
                bc16 = lambda t: t.t[:, :].unsqueeze(2).to_broadcast([128, 16, 64])
                P.op("dve", lambda e: e.tensor_tensor(out=y3, in0=y3, in1=bc16(s1), op=ALU.subtract), reads=[y.b, s1.b], writes=[y.b])
                P.op("dve", lambda e: e.tensor_tensor(out=y3, in0=y3, in1=bc16(s2), op=ALU.mult), reads=[y.b, s2.b], writes=[y.b])
                P.op("pool", lambda e: e.tensor_tensor(out=y.t[:, :], in0=y.t[:, :], in1=bc.t[:, 0, :], op=ALU.mult),
                     reads=[y.b, bc.b], writes=[y.b])
                P.op("pool", lambda e: e.tensor_tensor(out=y.t[:, :], in0=y.t[:, :], in1=bc.t[:, 1, :], op=ALU.add),
                     reads=[y.b, bc.b], writes=[y.b])
                P.op("dve", lambda e: e.tensor_tensor(out=y.t[:, :], in0=y.t[:, :], in1=bo.t[:, :], op=ALU.add),
                     reads=[y.b, bo.b], writes=[y.b])
                P.op("dve", lambda e: e.tensor_tensor(out=y.t[:, :], in0=y.t[:, :], in1=gt.t[:, :], op=ALU.mult),
                     reads=[y.b, gt.b], writes=[y.b])
                outproj_ln_tile(P, y.t[:, :], y.b, xr.t[:, :], xr.b, wo, zT, ptr, pdn, ident, stats, mv, rstd, g_bc, b_bc)
                P.dma("sp", tok_ap(xout, b, t0, S), xr.t[:, :], reads=[xr.b], writes=[xout["bufs"][b][tix]])
        P.barrier()


NEG = -1.0e30


def t5_bucket_np(n):
    n = np.asarray(n, dtype=np.int64)
    nf = np.maximum(n, 1).astype(np.float32)
    large = 16 + (np.log(nf / np.float32(16)) / np.float32(math.log(128 / 16)) * np.float32(16)).astype(np.int32)
    large = np.minimum(large, 31)
    return np.where(n < 16, n, large)


def host_consts():
    oh = np.zeros((32, 384), np.float32)
    for q in range(256):
        oh[int(t5_bucket_np(255 - q)), q] = 1.0
    cm = np.zeros((128, 128), np.float32)
    cm[np.triu_indices(128, 1)] = NEG
    ii, tt_ = np.meshgrid(np.arange(128), np.arange(128), indexing="ij")
    tri2 = ((ii // 64 == tt_ // 64) & (ii <= tt_)).astype(np.float32)
    s_, t_ = np.meshgrid(np.arange(64), np.arange(64), indexing="ij")
    masks = np.stack([(s_ < t_), (s_ > t_), (s_ <= t_)], 1).astype(np.float32)
    return {"ident": np.eye(128, dtype=np.float32), "bucket_oh": oh, "causal_mask": cm, "tri2": tri2, "scan_masks": masks,
            "pow2": np.tile((2.0 ** -np.arange(32, dtype=np.float64)).astype(np.float32)[None, :], (128, 1))}


def dsa_phase(P, cfg, li, j, xin, xout, T, VD):
    S, NSEQ = cfg.S, cfg.NSEQ
    NCK = S // 128
    qk_scale = 64 ** -0.5
    with ExitStack() as ph:
        w_in = P.sb(ph, "w_in", [128, 8, 456], BF16)
        w_uq = P.sb(ph, "w_uq", [128, 2, D], BF16)
        w_qi = P.sb(ph, "w_qi", [128, 2, 512], BF16)
        w_uk = P.sb(ph, "w_uk", [128, 8, 128], BF16)
        w_uv = P.sb(ph, "w_uv", [128, 16, 64], BF16)
        wo = P.sb(ph, "wo", [128, 8, D], BF16)
        gq = P.sb(ph, "gq", [128, 256], F32)
        gkv = P.sb(ph, "gkv", [128, 128], F32)
        gki = P.sb(ph, "gki", [128, 2, 64], F32)
        g_bc = P.sb(ph, "g_bc", [128, D], F32)
        b_bc = P.sb(ph, "b_bc", [128, D], F32)
        ident = P.sb(ph, "ident", [128, 128], F32)
        identb = P.sb(ph, "identb", [128, 128], BF16)
        cmask = P.sb(ph, "cmask", [128, 128], F32)
        TB = P.sb(ph, "TB", [128, 16, 2, 128], F32)
        rb31 = P.sb(ph, "rb31", [128, 16], F32)
        rbs = P.sb(ph, "rbs", [32, 16], F32)
        ohs = P.sb(ph, "ohs", [32, 384], F32)
        vrs = P.sb(ph, "vrs", [16, 384], F32)
        ckvT = P.sb(ph, "ckvT", [128, S], BF16)
        ckv = P.sb(ph, "ckv", [128, NCK, 128], BF16)
        kiT = P.sb(ph, "kiT", [128, S], BF16)
        Xr = [P.sb(ph, "Xr%d" % i, [128, D], F32) for i in range(2)]
        xT = P.sb(ph, "xT", [128, 8, 128], BF16)
        HS = P.sb(ph, "HS", [128, 456], F32)
        CQ = P.sb(ph, "CQ", [128, 512], F32)
        WI = P.sb(ph, "WI", [128, 8], F32)
        sm = P.sb(ph, "sm", [128, 8], F32)
        cqT = P.sb(ph, "cqT", [128, 2, 128], BF16)
        qT = P.sb(ph, "qT", [128, 8, 128], BF16)
        qabsT = P.sb(ph, "qabsT", [128, 16, 128], BF16)
        qabsT2 = P.sb(ph, "qabsT2", [128, 16, 128], BF16)
        qidxT = P.sb(ph, "qidxT", [128, 4, 128], BF16)
        score = P.sb(ph, "score", [128, S], F32)
        work = P.sb(ph, "work", [128, S], F32)
        mask = P.sb(ph, "mask", [128, S], F32)
        mask2 = P.sb(ph, "mask2", [128, S], F32)
        relu = [P.sb(ph, "relu%d" % i, [128, 512], F32) for i in range(2)]
        m8 = P.sb(ph, "m8", [128, 8], F32)
        junk = P.sb(ph, "junk", [128, S], BF16)
        pw = P.sb(ph, "pw", [128, 32], F32)
        wtab = P.sb(ph, "wtab", [128, 32], F32)
        bw = P.sb(ph, "bw", [128, 1], F32)
        blo = P.sb(ph, "blo", [128, 1], F32)
        bnb = P.sb(ph, "bnb", [128, 1], F32)
        bss = P.sb(ph, "bss", [128, 1], F32)
        Lh = [P.sb(ph, "Lh%d" % i, [128, S], F32) for i in range(2)]
        Pm = [P.sb(ph, "Pm%d" % i, [128, S], BF16) for i in range(2)]
        pT = [P.sb(ph, "pT%d" % i, [128, NCK, 128], BF16) for i in range(2)]
        olT = [P.sb(ph, "olT%d" % i, [128, 128], BF16) for i in range(2)]
        mx = P.sb(ph, "mx", [128, 16], F32)
        rs = P.sb(ph, "rs", [128, 16], F32)
        O = P.sb(ph, "O", [128, D], F32)
        zT = P.sb(ph, "zT", [128, 8, 128], BF16)
        stats = P.sb(ph, "stats", [128, 2, 6], F32)
        mv = P.sb(ph, "mv", [128, 2], F32)
        rstd = P.sb(ph, "rstd", [128, 1], F32)
        ptr = P.ps(ph, "ptr", shape=(128, 1024), n=2)
        pb2 = P.ps(ph, "pb2")
        scg = [P.ps(ph, "scg%d" % i) for i in range(2)]
        ppT = P.ps(ph, "ppT", shape=(128, 1024), dt=BF16)
        po = P.ps(ph, "po", shape=(128, 1024))

        load_weight_bf16(P, w_in.t, w_in.b, T.get("dsa_w_in", j), 0, D, 456, 8)
        load_weight_bf16(P, w_uq.t, w_uq.b, T.get("dsa_w_uq", j), 0, 256, D, 2)
        load_weight_bf16(P, w_qi.t, w_qi.b, T.get("dsa_w_qidx", j), 0, 256, 512, 2)
        load_weight_bf16(P, wo.t, wo.b, T.get("dsa_w_o", j), 0, D, D, 8)
        for hp in range(8):
            src = AP(T.get("dsa_w_uk", j), hp * 2 * 64 * 128, [[128, 128], [1, 128]])
            P.dma("pool", w_uk.t[:, hp, :], src, writes=[w_uk.b])
        for h in range(16):
            src = AP(T.get("dsa_w_uv", j), h * 128 * 64, [[64, 128], [1, 64]])
            P.dma("pool", w_uv.t[:, h, :], src, writes=[w_uv.b])
        P.dma("sp", gq.t[:, :], bcast_rows(T.get("dsa_q_norm_g", j), 0, 256), writes=[gq.b])
        P.dma("sp", gkv.t[:, :], bcast_rows(T.get("dsa_kv_norm_g", j), 0, 128), writes=[gkv.b])
        P.dma("sp", gki.t[:, 0, :], bcast_rows(T.get("dsa_kidx_g", j), 0, 64), writes=[gki.b])
        P.dma("sp", gki.t[:, 1, :], bcast_rows(T.get("dsa_kidx_b", j), 0, 64), writes=[gki.b])
        P.dma("sp", g_bc.t[:, :], bcast_rows(T.get("ln_g", li), 0, D), writes=[g_bc.b])
        P.dma("sp", b_bc.t[:, :], bcast_rows(T.get("ln_b", li), 0, D), writes=[b_bc.b])
        P.dma("sp", ident.t[:, :], T.get("ident").ap(), writes=[ident.b])
        P.dma("sp", cmask.t[:, :], T.get("causal_mask").ap(), writes=[cmask.b])
        P.dma("sp", pw.t[:, :], T.get("pow2").ap(), writes=[pw.b])
        P.dma("sp", rb31.t[:, :], bcast_rows(T.get("rel_bias"), 31 * 16, 16), writes=[rb31.b])
        P.dma("sp", rbs.t[:, :], T.get("rel_bias").ap(), writes=[rbs.b])
        P.dma("sp", ohs.t[:, :], T.get("bucket_oh").ap(), writes=[ohs.b])
        P.op("act", lambda e: e.activation(out=identb.t[:, :], in_=ident.t[:, :], func=AF.Copy), reads=[ident.b], writes=[identb.b])
        P.op("pe", lambda e: e.matmul(pb2.t[0:16, 0:384], rbs.t[:, :], ohs.t[:, :], start=True, stop=True),
             reads=[rbs.b, ohs.b], writes=[pb2.b])
        P.op("act", lambda e: e.activation(out=vrs.t[:, :], in_=pb2.t[0:16, 0:384], func=AF.Copy), reads=[pb2.b], writes=[vrs.b])
        vdb = VD["bufs"][0][0]
        P.dma("sp", VD["t"].ap(), vrs.t[:, :], reads=[vrs.b], writes=[vdb])
        for ti in range(128):
            for d in range(2):
                src = AP(VD["t"], 255 - d * 128 - ti, [[0, 1], [384, 16], [1, 128]])
                P.dma("sp", TB.t[ti:ti + 1, :, d, :], src, reads=[vdb], writes=[TB.b])

        qab = [qabsT, qabsT2]
        msk = [mask, mask2]

        def stage_PI(b, i):
            t0 = i * 128
            L = (i + 1) * 128
            xr = Xr[i % 2]
            qabsT = qab[i % 2]
            P.dma("sp", xr.t[:, :], tok_ap(xin, b, t0, S), reads=[xin["bufs"][b][i]], writes=[xr.b])
            for kc in range(8):
                P.op("pe", lambda e, kc=kc: e.transpose(ptr.t[:, kc * 128:(kc + 1) * 128], xr.t[:, kc * 128:(kc + 1) * 128], ident.t[:, :]),
                     reads=[xr.b, ident.b], writes=[ptr.bs[kc // 4]])
                if kc % 4 == 3:
                    g4 = kc // 4
                    P.op("act", lambda e, g4=g4: e.activation(
                        out=xT.t[:, g4 * 4:(g4 + 1) * 4, :], in_=ptr.t[:, g4 * 512:(g4 + 1) * 512].rearrange("p (a b) -> p a b", a=4),
                        func=AF.Copy), reads=[ptr.bs[g4]], writes=[xT.b])
            for kc in range(8):
                P.op("pe", lambda e, kc=kc: e.matmul(pb2.t[:, 0:456], xT.t[:, kc, :], w_in.t[:, kc, :], start=(kc == 0), stop=(kc == 7)),
                     reads=[xT.b, w_in.b], writes=[pb2.b])
            P.op("act", lambda e: e.activation(out=HS.t[:, :], in_=pb2.t[:, 0:456], func=AF.Copy), reads=[pb2.b], writes=[HS.b])
            for (c0, w, gt, col) in ((0, 256, gq, 0), (256, 128, gkv, 1)):
                P.op("act", lambda e, c0=c0, w=w, col=col: e.activation(
                    out=work.t[:, 0:w], in_=HS.t[:, c0:c0 + w], func=AF.Square, accum_out=sm.t[:, col:col + 1]),
                    reads=[HS.b, work.b], writes=[work.b, sm.b])
                P.op("act", lambda e, w=w, col=col: e.activation(
                    out=sm.t[:, col:col + 1], in_=sm.t[:, col:col + 1], func=AF.Sqrt, bias=1e-6, scale=1.0 / w),
                    reads=[sm.b], writes=[sm.b])
                P.op("dve", lambda e, col=col: e.reciprocal(out=sm.t[:, col:col + 1], in_=sm.t[:, col:col + 1]), reads=[sm.b], writes=[sm.b])
                P.op("dve", lambda e, c0=c0, w=w, gt=gt, col=col: e.scalar_tensor_tensor(
                    out=CQ.t[:, c0:c0 + w], in0=HS.t[:, c0:c0 + w], scalar=sm.t[:, col:col + 1], in1=gt.t[:, :],
                    op0=ALU.mult, op1=ALU.mult), reads=[HS.b, sm.b, gt.b, CQ.b], writes=[CQ.b])
            P.op("dve", lambda e: e.bn_stats(out=stats.t[:, 0, :], in_=HS.t[:, 384:448]), reads=[HS.b], writes=[stats.b])
            P.op("dve", lambda e: e.bn_aggr(out=mv.t[:, :], in_=stats.t[:, 0, :]), reads=[stats.b], writes=[mv.b])
            P.op("act", lambda e: e.activation(out=rstd.t[:, :], in_=mv.t[:, 1:2], func=AF.Sqrt, bias=LN_EPS, scale=1.0),
                 reads=[mv.b], writes=[rstd.b])
            P.op("dve", lambda e: e.reciprocal(out=rstd.t[:, :], in_=rstd.t[:, :]), reads=[rstd.b], writes=[rstd.b])
            P.op("dve", lambda e: e.tensor_scalar(out=CQ.t[:, 384:448], in0=HS.t[:, 384:448], scalar1=mv.t[:, 0:1], scalar2=rstd.t[:, 0:1],
                                                  op0=ALU.subtract, op1=ALU.mult), reads=[HS.b, mv.b, rstd.b, CQ.b], writes=[CQ.b])
            P.op("dve", lambda e: e.tensor_tensor(out=CQ.t[:, 384:448], in0=CQ.t[:, 384:448], in1=gki.t[:, 0, :], op=ALU.mult),
                 reads=[CQ.b, gki.b], writes=[CQ.b])
            P.op("dve", lambda e: e.tensor_tensor(out=CQ.t[:, 384:448], in0=CQ.t[:, 384:448], in1=gki.t[:, 1, :], op=ALU.add),
                 reads=[CQ.b, gki.b], writes=[CQ.b])
            P.op("dve", lambda e: e.tensor_copy(out=CQ.t[:, 448:512], in_=CQ.t[:, 384:448]), reads=[CQ.b], writes=[CQ.b])
            P.op("dve", lambda e: e.tensor_scalar(out=WI.t[:, :], in0=HS.t[:, 448:456], scalar1=512 ** -0.5, scalar2=None, op0=ALU.mult),
                 reads=[HS.b], writes=[WI.b])
            P.op("act", lambda e, i=i: e.activation(out=ckv.t[:, i, :], in_=CQ.t[:, 256:384], func=AF.Copy), reads=[CQ.b], writes=[ckv.b])
            for q in range(4):
                P.op("pe", lambda e, q=q: e.transpose(ptr.t[:, q * 128:(q + 1) * 128], CQ.t[:, q * 128:(q + 1) * 128], ident.t[:, :]),
                     reads=[CQ.b, ident.b], writes=[ptr.bs[0]])
            P.op("act", lambda e: e.activation(out=cqT.t[:, :, :], in_=ptr.t[:, 0:256].rearrange("p (a b) -> p a b", a=2), func=AF.Copy),
                 reads=[ptr.bs[0]], writes=[cqT.b])
            P.op("act", lambda e, t0=t0: e.activation(out=ckvT.t[:, t0:t0 + 128], in_=ptr.t[:, 256:384], func=AF.Copy),
                 reads=[ptr.bs[0]], writes=[ckvT.b])
            P.op("act", lambda e, t0=t0: e.activation(out=kiT.t[:, t0:t0 + 128], in_=ptr.t[:, 384:512], func=AF.Copy),
                 reads=[ptr.bs[0]], writes=[kiT.b])
            for g4 in range(2):
                for q in range(4):
                    hp = g4 * 4 + q
                    for kc in range(2):
                        P.op("pe", lambda e, hp=hp, q=q, kc=kc: e.matmul(
                            ptr.t[:, 512 + q * 128:512 + (q + 1) * 128], w_uq.t[:, kc, hp * 128:(hp + 1) * 128], cqT.t[:, kc, :],
                            start=(kc == 0), stop=(kc == 1)), reads=[w_uq.b, cqT.b], writes=[ptr.bs[1]])
                P.op("act", lambda e, g4=g4: e.activation(
                    out=qT.t[:, g4 * 4:(g4 + 1) * 4, :], in_=ptr.t[:, 512:1024].rearrange("p (a b) -> p a b", a=4), func=AF.Copy),
                    reads=[ptr.bs[1]], writes=[qT.b])
            for q in range(4):
                for kc in range(2):
                    P.op("pe", lambda e, q=q, kc=kc: e.matmul(
                        ptr.t[:, q * 128:(q + 1) * 128], w_qi.t[:, kc, q * 128:(q + 1) * 128], cqT.t[:, kc, :],
                        start=(kc == 0), stop=(kc == 1)), reads=[w_qi.b, cqT.b], writes=[ptr.bs[0]])
            P.op("act", lambda e: e.activation(out=qidxT.t[:, :, :], in_=ptr.t[:, 0:512].rearrange("p (a b) -> p a b", a=4), func=AF.Copy),
                 reads=[ptr.bs[0]], writes=[qidxT.b])
            for g8 in range(2):
                for q in range(4):
                    for par in range(2):
                        h = g8 * 8 + q * 2 + par
                        p0 = par * 64
                        P.op("pe", lambda e, h=h, q=q, p0=p0, par=par: e.matmul(
                            ptr.t[:, par * 512 + q * 128:par * 512 + (q + 1) * 128], w_uk.t[p0:p0 + 64, h // 2, :],
                            qT.t[p0:p0 + 64, h // 2, :], start=True, stop=True), reads=[w_uk.b, qT.b], writes=[ptr.bs[par]])
                for par in range(2):
                    P.op("act", lambda e, g8=g8, par=par: e.activation(
                        out=sap(qabsT.t, (g8 * 8 + par) * 128, [[256, 4], [1, 128]]),
                        in_=ptr.t[:, par * 512:(par + 1) * 512].rearrange("p (a b) -> p a b", a=4), func=AF.Copy),
                        reads=[ptr.bs[par]], writes=[qabsT.b])
            nkb = (L + 511) // 512
            for kb in range(nkb):
                c0 = kb * 512
                cw_ = min(512, L - c0)
                for hi in range(8):
                    p0 = (hi % 2) * 64
                    pq = scg[hi % 2]
                    rl = relu[hi % 2]
                    P.op("pe", lambda e, hi=hi, p0=p0, pq=pq, c0=c0, cw_=cw_: e.matmul(
                        pq.t[:, 0:cw_], qidxT.t[p0:p0 + 64, hi // 2, :], kiT.t[p0:p0 + 64, c0:c0 + cw_], start=True, stop=True),
                        reads=[qidxT.b, kiT.b], writes=[pq.b])
                    P.op("act", lambda e, pq=pq, rl=rl, cw_=cw_: e.activation(out=rl.t[:, 0:cw_], in_=pq.t[:, 0:cw_], func=AF.Relu),
                         reads=[pq.b], writes=[rl.b])
                    if hi == 0:
                        P.op("dve", lambda e, rl=rl, c0=c0, cw_=cw_: e.tensor_scalar(
                            out=score.t[:, c0:c0 + cw_], in0=rl.t[:, 0:cw_], scalar1=WI.t[:, 0:1], scalar2=None, op0=ALU.mult),
                            reads=[rl.b, WI.b, score.b], writes=[score.b])
                    else:
                        P.op("dve", lambda e, rl=rl, c0=c0, cw_=cw_, hi=hi: e.scalar_tensor_tensor(
                            out=score.t[:, c0:c0 + cw_], in0=rl.t[:, 0:cw_], scalar=WI.t[:, hi:hi + 1], in1=score.t[:, c0:c0 + cw_],
                            op0=ALU.mult, op1=ALU.add), reads=[rl.b, WI.b, score.b], writes=[score.b])
            P.op("dve", lambda e, t0=t0: e.tensor_tensor(out=score.t[:, t0:t0 + 128], in0=score.t[:, t0:t0 + 128], in1=cmask.t[:, :], op=ALU.add),
                 reads=[score.b, cmask.b], writes=[score.b])

        def topk_ops(b, i):
            L = (i + 1) * 128
            mask = msk[i % 2]
            ops = []
            if L <= 256:
                ops.append(lambda: P.op("dve", lambda e: e.tensor_scalar(out=mask.t[:, 0:L], in0=score.t[:, 0:L], scalar1=-1.0e29, scalar2=None, op0=ALU.is_ge),
                                        reads=[score.b, mask.b], writes=[mask.b]))
                return ops
            NIT = 26
            ops.append(lambda: P.op("dve", lambda e: e.tensor_scalar(out=work.t[:, 0:L], in0=score.t[:, 0:L], scalar1=-1.0e29, scalar2=0.0,
                                                                     op0=ALU.is_ge, op1=ALU.add), reads=[score.b, work.b], writes=[work.b]))
            ops.append(lambda: P.op("dve", lambda e: e.tensor_tensor(out=work.t[:, 0:L], in0=work.t[:, 0:L], in1=score.t[:, 0:L], op=ALU.mult),
                                    reads=[score.b, work.b], writes=[work.b]))
            ops.append(lambda: P.op("dve", lambda e: e.tensor_reduce(out=bw.t[:, 0:1], in_=work.t[:, 0:L], axis=AX.X, op=ALU.max, apply_absolute_value=True),
                                    reads=[work.b, bw.b], writes=[bw.b]))
            ops.append(lambda: P.op("dve", lambda e: e.tensor_scalar(out=bw.t[:, 0:1], in0=bw.t[:, 0:1], scalar1=1.0, scalar2=None, op0=ALU.add),
                                    reads=[bw.b], writes=[bw.b]))
            ops.append(lambda: P.op("dve", lambda e: e.tensor_scalar(out=wtab.t[:, :], in0=pw.t[:, :], scalar1=bw.t[:, 0:1], scalar2=None, op0=ALU.mult),
                                    reads=[pw.b, bw.b, wtab.b], writes=[wtab.b]))
            ops.append(lambda: P.op("dve", lambda e: e.tensor_scalar(out=blo.t[:, :], in0=bw.t[:, 0:1], scalar1=-1.0, scalar2=None, op0=ALU.mult),
                                    reads=[bw.b, blo.b], writes=[blo.b]))
            ops.append(lambda: P.op("dve", lambda e: e.scalar_tensor_tensor(out=bnb.t[:, :], in0=blo.t[:, :], scalar=-1.0, in1=wtab.t[:, 0:1],
                                                                           op0=ALU.mult, op1=ALU.subtract), reads=[blo.b, wtab.b, bnb.b], writes=[bnb.b]))
            for n in range(NIT):
                ops.append(lambda: P.op("act", lambda e: e.activation(out=junk.t[:, 0:L], in_=score.t[:, 0:L], func=AF.Sign, bias=bnb.t[:, 0:1], scale=1.0,
                                                                      accum_out=bss.t[:, 0:1]), reads=[score.b, bnb.b, junk.b, bss.b], writes=[junk.b, bss.b]))
                ops.append(lambda: P.op("dve", lambda e: e.tensor_scalar(out=bss.t[:, 0:1], in0=bss.t[:, 0:1], scalar1=float(511 - L) - 0.25, scalar2=None, op0=ALU.is_ge),
                                        reads=[bss.b], writes=[bss.b]))
                ops.append(lambda n=n: P.op("dve", lambda e: e.scalar_tensor_tensor(out=blo.t[:, :], in0=bss.t[:, 0:1], scalar=wtab.t[:, n:n + 1], in1=blo.t[:, :],
                                                                                   op0=ALU.mult, op1=ALU.add), reads=[bss.b, wtab.b, blo.b], writes=[blo.b]))
                if n + 1 < NIT:
                    ops.append(lambda n=n: P.op("dve", lambda e: e.scalar_tensor_tensor(out=bnb.t[:, :], in0=blo.t[:, :], scalar=-1.0, in1=wtab.t[:, n + 1:n + 2],
                                                                                       op0=ALU.mult, op1=ALU.subtract), reads=[blo.b, wtab.b, bnb.b], writes=[bnb.b]))
            ops.append(lambda: P.op("dve", lambda e: e.tensor_scalar(out=mask.t[:, 0:L], in0=score.t[:, 0:L], scalar1=blo.t[:, 0:1], scalar2=None, op0=ALU.is_ge),
                                    reads=[score.b, blo.b, mask.b], writes=[mask.b]))
            return ops

        def stage_H(b, i, side):
            t0 = i * 128
            L = (i + 1) * 128
            nkb = (L + 511) // 512
            qabsT = qab[i % 2]
            mask = msk[i % 2]
            per_head = (len(side) + 15) // 16
            nfar = max(0, i - 1)
            for h in range(16):
                lh, pm, pt, ol = Lh[h % 2], Pm[h % 2], pT[h % 2], olT[h % 2]
                for kb in range(nkb):
                    c0 = kb * 512
                    cw_ = min(512, L - c0)
                    pq = scg[kb % 2]
                    P.op("pe", lambda e, h=h, pq=pq, c0=c0, cw_=cw_: e.matmul(
                        pq.t[:, 0:cw_], qabsT.t[:, h, :], ckvT.t[:, c0:c0 + cw_], start=True, stop=True),
                        reads=[qabsT.b, ckvT.b], writes=[pq.b])
                    for ck in range(c0 // 128, (c0 + cw_) // 128):
                        lo = ck * 128 - c0
                        if ck < nfar:
                            continue
                        d = i - ck
                        P.op("dve", lambda e, pq=pq, lo=lo, ck=ck, d=d, h=h, lh=lh: e.scalar_tensor_tensor(
                            out=lh.t[:, ck * 128:(ck + 1) * 128], in0=pq.t[:, lo:lo + 128], scalar=qk_scale, in1=TB.t[:, h, d, :],
                            op0=ALU.mult, op1=ALU.add), reads=[pq.b, TB.b, lh.b], writes=[lh.b])
                    f1 = min(c0 + cw_, nfar * 128)
                    if f1 > c0:
                        P.op("dve", lambda e, pq=pq, c0=c0, f1=f1, h=h, lh=lh: e.tensor_scalar(
                            out=lh.t[:, c0:f1], in0=pq.t[:, 0:f1 - c0], scalar1=qk_scale, scalar2=rb31.t[:, h:h + 1],
                            op0=ALU.mult, op1=ALU.add), reads=[pq.b, rb31.b, lh.b], writes=[lh.b])
                P.op("dve", lambda e, h=h, lh=lh, L=L: e.tensor_reduce(out=mx.t[:, h:h + 1], in_=lh.t[:, 0:L], axis=AX.X, op=ALU.max, negate=True),
                     reads=[lh.b, mx.b], writes=[mx.b])
                P.op("act", lambda e, h=h, lh=lh, L=L: e.activation(out=lh.t[:, 0:L], in_=lh.t[:, 0:L], func=AF.Exp, bias=mx.t[:, h:h + 1], scale=1.0),
                     reads=[lh.b, mx.b], writes=[lh.b])
                P.op("dve", lambda e, h=h, lh=lh, pm=pm, L=L: e.scalar_tensor_tensor(
                    out=pm.t[:, 0:L], in0=lh.t[:, 0:L], scalar=1.0, in1=mask.t[:, 0:L], op0=ALU.mult, op1=ALU.mult,
                    accum_out=rs.t[:, h:h + 1]), reads=[lh.b, mask.b, pm.b, rs.b], writes=[pm.b, rs.b])
                for ck in range(i + 1):
                    P.op("pe", lambda e, ck=ck, pm=pm: e.transpose(ppT.t[:, (ck % 8) * 128:(ck % 8 + 1) * 128], pm.t[:, ck * 128:(ck + 1) * 128], identb.t[:, :]),
                         reads=[pm.b, identb.b], writes=[ppT.b])
                    if ck % 8 == 7 or ck == i:
                        c8 = (ck // 8) * 8
                        nn = ck - c8 + 1
                        P.op("act", lambda e, c8=c8, nn=nn, pt=pt: e.activation(
                            out=pt.t[:, c8:c8 + nn, :], in_=ppT.t[:, 0:nn * 128].rearrange("p (a b) -> p a b", a=nn), func=AF.Copy),
                            reads=[ppT.b, pt.b], writes=[pt.b])
                for ck in range(i + 1):
                    P.op("pe", lambda e, ck=ck, pt=pt: e.matmul(pb2.t[:, 0:128], ckv.t[:, ck, :], pt.t[:, ck, :], start=(ck == 0), stop=(ck == i)),
                         reads=[ckv.b, pt.b], writes=[pb2.b])
                P.op("act", lambda e, ol=ol: e.activation(out=ol.t[:, :], in_=pb2.t[:, 0:128], func=AF.Copy), reads=[pb2.b, ol.b], writes=[ol.b])
                P.op("pe", lambda e, h=h, ol=ol: e.matmul(po.t[:, h * 64:(h + 1) * 64], ol.t[:, :], w_uv.t[:, h, :], start=True, stop=True),
                     reads=[ol.b, w_uv.b], writes=[po.b])
                for _ in range(per_head):
                    if side:
                        side.pop(0)()
            while side:
                side.pop(0)()

        def stage_O(b, i):
            t0 = i * 128
            xr = Xr[i % 2]
            P.op("dve", lambda e: e.reciprocal(out=rs.t[:, :], in_=rs.t[:, :]), reads=[rs.b], writes=[rs.b])
            for hf in range(2):
                P.op("dve", lambda e, hf=hf: e.tensor_tensor(
                    out=O.t[:, hf * 512:(hf + 1) * 512].rearrange("p (h n) -> p h n", h=8),
                    in0=po.t[:, hf * 512:(hf + 1) * 512].rearrange("p (h n) -> p h n", h=8),
                    in1=rs.t[:, hf * 8:(hf + 1) * 8].unsqueeze(2).to_broadcast([128, 8, 64]), op=ALU.mult),
                    reads=[po.b, rs.b, O.b], writes=[O.b])
            outproj_ln_tile(P, O.t[:, :], O.b, xr.t[:, :], xr.b, wo, zT, ptr, scg, ident, stats, mv, rstd, g_bc, b_bc)
            P.dma("sp", tok_ap(xout, b, t0, S), xr.t[:, :], reads=[xr.b], writes=[xout["bufs"][b][i]])

        for b in range(NSEQ):
            stage_PI(b, 0)
            for f in topk_ops(b, 0):
                f()
            for i in range(NCK):
                side = []
                if i + 1 < NCK:
                    stage_PI(b, i + 1)
                    side = topk_ops(b, i + 1)
                stage_H(b, i, side)
                stage_O(b, i)
        P.barrier()


INPUT_SPECS = [
    ("x", None), ("ln_g", (4, 2, 1024)), ("ln_b", (4, 2, 1024)), ("rwkv_mix", (2, 6, 1024)),
    ("rwkv_w_rkv", (2, 3, 1024, 1024)), ("rwkv_w0", (2, 1024)), ("rwkv_w1", (2, 1024, 64)),
    ("rwkv_w2", (2, 64, 1024)), ("rwkv_a0", (2, 1024)), ("rwkv_a1", (2, 1024, 64)), ("rwkv_a2", (2, 64, 1024)),
    ("rwkv_v0", (1, 1024)), ("rwkv_v1", (1, 1024, 32)), ("rwkv_v2", (1, 32, 1024)), ("rwkv_g1", (2, 1024, 160)),
    ("rwkv_g2", (2, 160, 1024)), ("rwkv_k_k", (2, 1024)), ("rwkv_k_a", (2, 1024)), ("rwkv_r_k", (2, 16, 64)),
    ("rwkv_lnx_g", (2, 1024)), ("rwkv_lnx_b", (2, 1024)), ("rwkv_w_o", (2, 1024, 1024)),
    ("dsa_w_in", (2, 1024, 456)), ("dsa_q_norm_g", (2, 256)), ("dsa_kv_norm_g", (2, 128)),
    ("dsa_w_uq", (2, 256, 1024)), ("dsa_w_uk", (2, 16, 64, 128)), ("dsa_w_uv", (2, 16, 128, 64)),
    ("dsa_w_qidx", (2, 256, 512)), ("dsa_kidx_g", (2, 64)), ("dsa_kidx_b", (2, 64)), ("dsa_w_o", (2, 1024, 1024)),
    ("rel_bias", (32, 16)), ("ffn_w_up", (4, 1024, 5632)), ("ffn_conv_w", (4, 3, 5632)), ("ffn_conv_b", (4, 5632)),
    ("ffn_w_down", (4, 2816, 1024)),
]


CONST_SHAPES = {"ident": (128, 128), "bucket_oh": (32, 384), "causal_mask": (128, 128), "tri2": (128, 128), "scan_masks": (64, 3, 64), "pow2": (128, 32)}


class Tensors:
    def __init__(self, nc):
        self.nc = nc
        self.d = {}
        self.shapes = dict(INPUT_SPECS)

    def get(self, name, j=None):
        key = name if j is None else "%s_%d" % (name, j)
        if key not in self.d:
            if name in CONST_SHAPES:
                shp = list(CONST_SHAPES[name])
            else:
                shp = list(self.shapes[name])
                if j is not None:
                    shp = shp[1:]
            self.d[key] = self.nc.dram_tensor(key, shp, F32, kind="ExternalInput")
        return self.d[key]


def host_inputs(names, inputs):
    out = {}
    consts = host_consts()
    for key in names:
        if key in consts:
            out[key] = consts[key]
        elif key in inputs:
            out[key] = np.ascontiguousarray(inputs[key], dtype=np.float32)
        else:
            name, j = key.rsplit("_", 1)
            out[key] = np.ascontiguousarray(inputs[name][int(j)], dtype=np.float32)
    return out


def dram_stream(nc, name, cfg, kind="Internal"):
    t = nc.dram_tensor(name, [cfg.NSEQ, cfg.S, D], F32, kind=kind)
    return {"t": t, "bufs": [[Buf() for _ in range(cfg.S // 128)] for _ in range(cfg.NSEQ)]}


def sc_stream(nc, name, cfg, kind="Internal"):
    t = nc.dram_tensor(name, [cfg.NSEQ * cfg.S + 1, D], F32, kind=kind)
    return {"t": t, "bufs": [[Buf() for _ in range(cfg.S // 128)] for _ in range(cfg.NSEQ)]}


def build(cfg):
    nc = bass.Bass("TRN2", target_bir_lowering=False)
    T = Tensors(nc)
    dbg = getattr(cfg, "debug_out", "all")
    xin = dram_stream(nc, "x", cfg, kind="ExternalInput")
    out = dram_stream(nc, "out", cfg, kind="ExternalOutput")
    stages = []
    for li in cfg.layers:
        if "mix" in cfg.parts:
            stages.append(("mix", li))
        if "ffn" in cfg.parts:
            stages.append(("ffn", li))
    with ExitStack() as es:
        P = Prog(nc, es)
        cur = xin
        SC = None
        VD = None
        for si, (kind, li) in enumerate(stages):
            last = si == len(stages) - 1
            nxt = out if last else dram_stream(nc, "xs%d" % si, cfg)
            if kind == "ffn":
                ffn_phase(P, cfg, li, cur, nxt, T)
            elif li % 2 == 0:
                j = li // 2
                if SC is None:
                    dk = "ExternalOutput" if dbg in ("pre", "scan") else "Internal"
                    SC = {k: sc_stream(nc, ("dbg_" + k) if dk == "ExternalOutput" else ("sc_" + k), cfg, kind=dk)
                          for k in ("k", "v", "b", "g", "bonus", "y", "fa", "fb", "fk", "fr", "wc")}
                    if j > 0:
                        SC["vfirst"] = sc_stream(nc, "vfirst_in", cfg, kind="ExternalInput")
                        T.d["vfirst_in"] = SC["vfirst"]["t"]
                    else:
                        SC["vfirst"] = sc_stream(nc, "sc_vfirst", cfg)
                rwkv_pre_phase(P, cfg, li, j, cur, T, SC)
                if dbg == "pre":
                    break
                rwkv_scan_phase(P, cfg, SC, T)
                if dbg == "scan":
                    break
                rwkv_post_phase(P, cfg, li, j, cur, nxt, T, SC)
            else:
                if VD is None:
                    VD = {"t": nc.dram_tensor("sc_vr", [16, 384], F32), "bufs": [[Buf()]]}
                dsa_phase(P, cfg, li, li // 2, cur, nxt, T, VD)
            cur = nxt
        P.barrier()
        print("instructions emitted:", P.ninst, "sems:", P.nsem, flush=True)
    return nc, list(T.d.keys())


def kernel(**inputs):
    cfg = Cfg()
    nc, names = build(cfg)
    x = np.ascontiguousarray(inputs["x"], dtype=np.float32)
    ncores = 8
    shards = np.split(x, ncores, axis=0)
    base = host_inputs(names, inputs)
    in_maps = []
    for c in range(ncores):
        m = dict(base)
        m["x"] = shards[c]
        in_maps.append(m)
    res = run_bass_kernel_spmd(nc, in_maps, core_ids=list(range(ncores)))
    return np.concatenate([r["out"] for r in res.results], axis=0)
```

```python
import math
from contextlib import ExitStack

import numpy as np
import concourse.bass as bass
import concourse.mybir as mybir
from concourse.ap import AP
from concourse.bass_utils import run_bass_kernel_spmd

F32 = mybir.dt.float32
BF16 = mybir.dt.bfloat16
AF = mybir.ActivationFunctionType
ALU = mybir.AluOpType
AX = mybir.AxisListType

D = 1024
DEPTH = 4
DFF = 2816
NCH = 44
NFC = 22
DN_ALPHA = (2 * DEPTH) ** 0.25
LN_EPS = 1e-5
SEM_ROT = 30000


class Cfg:
    def __init__(self, S=2048, NSEQ=4, TB=512, layers=(0, 1, 2, 3), parts=("mix", "ffn")):
        self.S = S
        self.NSEQ = NSEQ
        self.TB = TB
        self.layers = tuple(layers)
        self.parts = tuple(parts)


class Buf:
    __slots__ = ("w", "r")

    def __init__(self):
        self.w = None
        self.r = {}


class Tile:
    def __init__(self, t, n=1):
        self.t = t
        self.bs = [Buf() for _ in range(n)]
        self.b = self.bs[0]


class Prog:
    def __init__(self, nc, es):
        self.nc = nc
        self.es = es
        self.E = {"pe": nc.tensor, "act": nc.scalar, "dve": nc.vector, "pool": nc.gpsimd, "sp": nc.sync}
        self.sems = {}
        self.cnt = {}
        self.cur = {}
        self.nsem = 0
        for k in ("pe", "act", "dve", "pool"):
            self._newsem(k)
        self.dmaq = {"sp": [], "pool": []}
        self.dmai = {"sp": 0, "pool": 0}
        for q in ("sp", "pool"):
            for i in range(8):
                slot = "d_%s%d" % (q, i)
                self._newsem(slot)
                self.dmaq[q].append(slot)
        self.seen = {e: {} for e in self.E}
        self.ninst = 0
        self.nalloc = 0

    def _newsem(self, slot):
        self.nsem += 1
        key = "%s#%d" % (slot, self.nsem)
        self.sems[key] = self.es.enter_context(self.nc.semaphore("s%d" % self.nsem))
        self.cnt[key] = 0
        self.cur[slot] = key
        return key

    def _wait(self, eng, ev):
        key, val = ev
        if self.seen[eng].get(key, 0) >= val:
            return
        self.E[eng].wait_ge(self.sems[key], val)
        self.seen[eng][key] = val
        self.ninst += 1

    def _deps(self, eng, reads, writes):
        mykey = self.cur.get(eng)
        for b in reads:
            if b.w is not None:
                if eng == "pe" and b.w[0] == mykey:
                    continue
                self._wait(eng, b.w)
        for b in writes:
            if b.w is not None and not (eng == "pe" and b.w[0] == mykey):
                self._wait(eng, b.w)
            for k, v in b.r.items():
                if k == mykey:
                    continue
                self._wait(eng, (k, v))

    def _mark(self, ev, reads, writes):
        for b in reads:
            if b.r.get(ev[0], 0) < ev[1]:
                b.r[ev[0]] = ev[1]
        for b in writes:
            b.w = ev
            b.r = {}

    def op(self, eng, fn, reads=(), writes=()):
        self._deps(eng, reads, writes)
        key = self.cur[eng]
        if self.cnt[key] >= SEM_ROT:
            key = self._newsem(eng)
        ins = fn(self.E[eng])
        self.cnt[key] += 1
        ins.then_inc(self.sems[key], 1)
        self.ninst += 1
        self._mark((key, self.cnt[key]), reads, writes)

    def dma(self, q, out, in_, reads=(), writes=(), **kw):
        self._deps(q, reads, writes)
        i = self.dmai[q]
        self.dmai[q] = (i + 1) % len(self.dmaq[q])
        slot = self.dmaq[q][i]
        key = self.cur[slot]
        self._wait(q, (key, self.cnt[key]))
        if self.cnt[key] >= SEM_ROT:
            key = self._newsem(slot)
        ins = self.E[q].dma_start(out=out, in_=in_, **kw)
        self.cnt[key] += 16
        ins.then_inc(self.sems[key], 16)
        self.ninst += 1
        self._mark((key, self.cnt[key]), reads, writes)

    def barrier(self, engines=("pe", "act", "dve", "pool", "sp")):
        evs = [(k, v) for k, v in self.cnt.items() if v > 0]
        for e in engines:
            for ev in evs:
                if ev[0] == self.cur.get(e):
                    continue
                self._wait(e, ev)

    def sb(self, st, name, shape, dt, n=1):
        self.nalloc += 1
        return Tile(st.enter_context(self.nc.sbuf_tensor("sb%d_%s" % (self.nalloc, name), shape, dt)), n)

    def ps(self, st, name, shape=(128, 512), dt=F32, n=1):
        self.nalloc += 1
        return Tile(st.enter_context(self.nc.psum_tensor("ps%d_%s" % (self.nalloc, name), list(shape), dt)), n)


def bcast_rows(ap1d_tensor, offset, n, parts=128):
    return AP(ap1d_tensor, offset, [[0, parts], [1, n]])


def layer_norm_rows(P, Y, yb, stats, mv, rstd, g_bc, b_bc, out_eng="pool"):
    P.op("dve", lambda e: e.bn_stats(out=stats.t[:, 0, :], in_=Y[:, 0:512]), reads=[yb], writes=[stats.b])
    P.op("dve", lambda e: e.bn_stats(out=stats.t[:, 1, :], in_=Y[:, 512:1024]), reads=[yb], writes=[stats.b])
    P.op("dve", lambda e: e.bn_aggr(out=mv.t[:, :], in_=stats.t[:, :, :].rearrange("p a b -> p (a b)")),
         reads=[stats.b], writes=[mv.b])
    P.op("act", lambda e: e.activation(out=rstd.t[:, :], in_=mv.t[:, 1:2], func=AF.Sqrt, bias=LN_EPS, scale=1.0),
         reads=[mv.b], writes=[rstd.b])
    P.op("dve", lambda e: e.reciprocal(out=rstd.t[:, :], in_=rstd.t[:, :]), reads=[rstd.b], writes=[rstd.b])
    P.op("dve", lambda e: e.tensor_scalar(out=Y, in0=Y, scalar1=mv.t[:, 0:1], scalar2=rstd.t[:, 0:1],
                                          op0=ALU.subtract, op1=ALU.mult), reads=[yb, mv.b, rstd.b], writes=[yb])
    P.op(out_eng, lambda e: e.tensor_tensor(out=Y, in0=Y, in1=g_bc.t[:, :], op=ALU.mult), reads=[yb, g_bc.b], writes=[yb])
    P.op(out_eng, lambda e: e.tensor_tensor(out=Y, in0=Y, in1=b_bc.t[:, :], op=ALU.add), reads=[yb, b_bc.b], writes=[yb])


def load_weight_bf16(P, dst, dstb, src_t, src_off, K, N, kchunks):
    CB = 1024
    for kc in range(kchunks):
        rows = min(128, K - kc * 128)
        for c0 in range(0, N, CB):
            c1 = min(N, c0 + CB)
            src = AP(src_t, src_off + kc * 128 * N + c0, [[N, rows], [1, c1 - c0]])
            P.dma("pool", dst[0:rows, kc, c0:c1], src, writes=[dstb])


def ffn_phase(P, cfg, li, xin, xout, T):
    nc = P.nc
    S, NSEQ, TB = cfg.S, cfg.NSEQ, cfg.TB
    NT = TB // 128
    with ExitStack() as ph:
        wup = P.sb(ph, "wup", [128, 8, 2 * DFF], BF16)
        wd = P.sb(ph, "wd", [128, NFC, D], BF16)
        X = P.sb(ph, "X", [128, NT, D], F32, n=NT)
        xT = P.sb(ph, "xT", [128, 8, TB], BF16, n=8)
        hT = P.sb(ph, "hT", [128, NFC, TB], BF16, n=NFC)
        U = [P.sb(ph, "U%d" % i, [128, TB + 2], F32) for i in range(2)]
        acc = [P.sb(ph, "acc%d" % i, [128, TB], F32) for i in range(2)]
        G = P.sb(ph, "G", [128, TB], F32)
        Cc = P.sb(ph, "Cc", [128, NCH, 2], F32, n=NCH)
        cw = P.sb(ph, "cw", [128, 4, NCH], F32)
        g_bc = P.sb(ph, "g_bc", [128, D], F32)
        b_bc = P.sb(ph, "b_bc", [128, D], F32)
        ident = P.sb(ph, "ident", [128, 128], F32)
        stats = P.sb(ph, "stats", [128, 2, 6], F32)
        mv = P.sb(ph, "mv", [128, 2], F32)
        rstd = P.sb(ph, "rstd", [128, 1], F32)
        ptr = P.ps(ph, "ptr")
        pup = [P.ps(ph, "pup%d" % i) for i in range(2)]
        pdn = [P.ps(ph, "pdn%d" % i) for i in range(2)]

        load_weight_bf16(P, wup.t, wup.b, T.get("ffn_w_up", li), 0, D, 2 * DFF, 8)
        load_weight_bf16(P, wd.t, wd.b, T.get("ffn_w_down", li), 0, DFF, D, NFC)
        for j in range(3):
            src = AP(T.get("ffn_conv_w", li), j * 2 * DFF, [[1, 128], [128, NCH]])
            P.dma("sp", cw.t[:, j, :], src, writes=[cw.b], allow_slow_non_contiguous=True)
        src = AP(T.get("ffn_conv_b", li), 0, [[1, 128], [128, NCH]])
        P.dma("sp", cw.t[:, 3, :], src, writes=[cw.b], allow_slow_non_contiguous=True)
        P.dma("sp", g_bc.t[:, :], bcast_rows(T.get("ln_g", li), D, D), writes=[g_bc.b])
        P.dma("sp", b_bc.t[:, :], bcast_rows(T.get("ln_b", li), D, D), writes=[b_bc.b])
        P.dma("sp", ident.t[:, :], T.get("ident").ap(), writes=[ident.b])

        for b in range(NSEQ):
            for ch in range(NCH):
                P.op("pool", lambda e, ch=ch: e.memset(Cc.t[:, ch, :], 0.0), writes=[Cc.bs[ch]])
            for blk in range(S // TB):
                t0 = blk * TB
                for tt in range(NT):
                    src = AP(xin["t"], (b * S + t0 + tt * 128) * D, [[D, 128], [1, D]])
                    P.dma("sp", X.t[:, tt, :], src, reads=[xin["bufs"][b][(t0 // 128) + tt]], writes=[X.bs[tt]])
                for kc in range(8):
                    for tt in range(NT):
                        P.op("pe", lambda e, kc=kc, tt=tt: e.transpose(
                            ptr.t[:, tt * 128:(tt + 1) * 128], X.t[:, tt, kc * 128:(kc + 1) * 128], ident.t[:, :]),
                            reads=[X.bs[tt], ident.b], writes=[ptr.b])
                    P.op("act", lambda e, kc=kc: e.activation(out=xT.t[:, kc, :], in_=ptr.t[:, 0:TB], func=AF.Copy),
                         reads=[ptr.b], writes=[xT.bs[kc]])
                for fc in range(NFC):
                    for gi, ch in enumerate((fc, fc + NFC)):
                        pp = pup[gi]
                        for kc in range(8):
                            P.op("pe", lambda e, kc=kc, ch=ch, pp=pp: e.matmul(
                                pp.t[:, 0:TB], wup.t[:, kc, ch * 128:(ch + 1) * 128], xT.t[:, kc, :],
                                start=(kc == 0), stop=(kc == 7)),
                                reads=[wup.b, xT.bs[kc]], writes=[pp.b])
                        Ui = U[gi]
                        P.op("pool", lambda e, ch=ch, Ui=Ui: e.tensor_copy(out=Ui.t[:, 0:2], in_=Cc.t[:, ch, :]),
                             reads=[Cc.bs[ch]], writes=[Ui.b])
                        P.op("act", lambda e, pp=pp, Ui=Ui: e.activation(out=Ui.t[:, 2:TB + 2], in_=pp.t[:, 0:TB], func=AF.Copy),
                             reads=[pp.b], writes=[Ui.b])
                        P.op("pool", lambda e, ch=ch, Ui=Ui: e.tensor_copy(out=Cc.t[:, ch, :], in_=Ui.t[:, TB:TB + 2]),
                             reads=[Ui.b], writes=[Cc.bs[ch]])
                        a = acc[gi]
                        P.op("dve", lambda e, ch=ch, Ui=Ui, a=a: e.tensor_scalar(
                            out=a.t[:, :], in0=Ui.t[:, 2:TB + 2], scalar1=cw.t[:, 2, ch:ch + 1], scalar2=cw.t[:, 3, ch:ch + 1],
                            op0=ALU.mult, op1=ALU.add), reads=[Ui.b, cw.b], writes=[a.b])
                        P.op("dve", lambda e, ch=ch, Ui=Ui, a=a: e.scalar_tensor_tensor(
                            out=a.t[:, :], in0=Ui.t[:, 1:TB + 1], scalar=cw.t[:, 1, ch:ch + 1], in1=a.t[:, :],
                            op0=ALU.mult, op1=ALU.add), reads=[Ui.b, cw.b, a.b], writes=[a.b])
                        P.op("dve", lambda e, ch=ch, Ui=Ui, a=a: e.scalar_tensor_tensor(
                            out=a.t[:, :], in0=Ui.t[:, 0:TB], scalar=cw.t[:, 0, ch:ch + 1], in1=a.t[:, :],
                            op0=ALU.mult, op1=ALU.add), reads=[Ui.b, cw.b, a.b], writes=[a.b])
                    P.op("act", lambda e: e.activation(out=G.t[:, :], in_=acc[0].t[:, :], func=AF.Silu),
                         reads=[acc[0].b], writes=[G.b])
                    P.op("dve", lambda e, fc=fc: e.tensor_tensor(out=hT.t[:, fc, :], in0=G.t[:, :], in1=acc[1].t[:, :], op=ALU.mult),
                         reads=[G.b, acc[1].b], writes=[hT.bs[fc]])
                for tt in range(NT):
                    for hf in range(2):
                        pp = pdn[hf]
                        for fc in range(NFC):
                            P.op("pe", lambda e, fc=fc, tt=tt, hf=hf, pp=pp: e.matmul(
                                pp.t[:, :], hT.t[:, fc, tt * 128:(tt + 1) * 128], wd.t[:, fc, hf * 512:(hf + 1) * 512],
                                start=(fc == 0), stop=(fc == NFC - 1)),
                                reads=[hT.bs[fc], wd.b], writes=[pp.b])
                        P.op("dve", lambda e, tt=tt, hf=hf, pp=pp: e.scalar_tensor_tensor(
                            out=X.t[:, tt, hf * 512:(hf + 1) * 512], in0=X.t[:, tt, hf * 512:(hf + 1) * 512],
                            scalar=DN_ALPHA, in1=pp.t[:, :], op0=ALU.mult, op1=ALU.add),
                            reads=[X.bs[tt], pp.b], writes=[X.bs[tt]])
                    layer_norm_rows(P, X.t[:, tt, :], X.bs[tt], stats, mv, rstd, g_bc, b_bc)
                    dst = AP(xout["t"], (b * S + t0 + tt * 128) * D, [[D, 128], [1, D]])
                    P.dma("sp", dst, X.t[:, tt, :], reads=[X.bs[tt]], writes=[xout["bufs"][b][(t0 // 128) + tt]])
        P.barrier()


def sap(t, off, dims, parts=128, p0=0):
    ps = t[:].ap[0][0]
    return AP(t, p0 * ps + off, [[ps, parts]] + [list(d) for d in dims])


def outproj_ln_tile(P, Z, zb, Xr, xrb, wo, zT, ptr, pdn, ident, stats, mv, rstd, g_bc, b_bc):
    for kc in range(8):
        P.op("pe", lambda e, kc=kc: e.transpose(ptr.t[:, kc * 128:(kc + 1) * 128],
                                                 Z[:, kc * 128:(kc + 1) * 128], ident.t[:, :]),
             reads=[zb, ident.b], writes=[ptr.bs[kc // 4]])
        if kc % 4 == 3:
            g4 = kc // 4
            P.op("act", lambda e, g4=g4: e.activation(
                out=zT.t[:, g4 * 4:(g4 + 1) * 4, :],
                in_=ptr.t[:, g4 * 512:(g4 + 1) * 512].rearrange("p (a b) -> p a b", a=4), func=AF.Copy),
                reads=[ptr.bs[g4]], writes=[zT.b])
    for hf in range(2):
        pp = pdn[hf]
        for kc in range(8):
            P.op("pe", lambda e, kc=kc, hf=hf, pp=pp: e.matmul(
                pp.t[:, :], zT.t[:, kc, :], wo.t[:, kc, hf * 512:(hf + 1) * 512], start=(kc == 0), stop=(kc == 7)),
                reads=[zT.b, wo.b], writes=[pp.b])
        P.op("dve", lambda e, hf=hf, pp=pp: e.scalar_tensor_tensor(
            out=Xr[:, hf * 512:(hf + 1) * 512], in0=Xr[:, hf * 512:(hf + 1) * 512], scalar=DN_ALPHA, in1=pp.t[:, :],
            op0=ALU.mult, op1=ALU.add), reads=[xrb, pp.b], writes=[xrb])
    layer_norm_rows(P, Xr, xrb, stats, mv, rstd, g_bc, b_bc)


def tok_ap(stream, b, t0, S, n=128):
    return AP(stream["t"], (b * S + t0) * D, [[D, n], [1, D]])


NBR = {"r": 0, "k": 1, "v": 2, "w": 3, "a": 4, "g": 5}


def rwkv_pre_phase(P, cfg, li, j, xin, T, SC):
    nc = P.nc
    S, NSEQ = cfg.S, cfg.NSEQ
    TBR = 128
    NT = TBR // 128
    vres = j > 0
    with ExitStack() as ph:
        wrkv = P.sb(ph, "wrkv", [128, 24, D], BF16)
        l1 = P.sb(ph, "l1", [128, 8, 320], BF16)
        l2 = P.sb(ph, "l2", [128, 5, D], BF16)
        bc = P.sb(ph, "bc", [128, 6, D], F32)
        mixT = P.sb(ph, "mixT", [128, 6, 8], F32)
        ident = P.sb(ph, "ident", [128, 128], F32)
        Xs = [P.sb(ph, "Xs%d" % i, [128, D], F32) for i in range(2)]
        xTf = P.sb(ph, "xTf", [128, 8, TBR + 1], F32)
        xxT = P.sb(ph, "xxT", [128, 8, TBR], F32)
        tmpm = [P.sb(ph, "tmpm%d" % i, [128, 8, TBR], F32) for i in range(2)]
        xm = [P.sb(ph, "xm%d" % i, [128, 8, TBR], BF16) for i in range(2)]
        hl = {k: P.sb(ph, "hl_" + k, [128, TBR], BF16) for k in ("w", "a", "g0", "g1", "v")}
        names = ["R", "K", "V", "ZW", "ZA", "G", "ZV", "KK", "TMP", "BT", "VF"]
        EP = P.sb(ph, "EP", [128, D], F32)
        EM = P.sb(ph, "EM", [128, D], F32)
        FT = [P.sb(ph, "FT%d" % i, [128, 4, 128], F32) for i in range(2)]
        tri2 = P.sb(ph, "tri2", [128, 128], F32)
        nft = 0
        tl = {k: [P.sb(ph, "%s%d" % (k, tt), [128, D], F32) for tt in range(NT)] for k in names}
        ss = P.sb(ph, "ss", [128, 16], F32)
        bs = P.sb(ph, "bs", [128, 16], F32)
        ptr = P.ps(ph, "ptr")
        pp2 = [[P.ps(ph, "pp%d%d" % (i, h)) for h in range(2)] for i in range(2)]
        pl = [P.ps(ph, "pl%d" % i) for i in range(2)]

        for br in range(3):
            load_weight_bf16(P, wrkv.t[:, br * 8:(br + 1) * 8, :], wrkv.b, T.get("rwkv_w_rkv", j), br * D * D, D, D, 8)
        for (nm, c0, w) in (("rwkv_w1", 0, 64), ("rwkv_a1", 64, 64), ("rwkv_g1", 128, 160)) + ((("rwkv_v1", 288, 32),) if vres else ()):
            jj = j - 1 if nm == "rwkv_v1" else j
            for kc in range(8):
                src = AP(T.get(nm, jj), kc * 128 * w, [[w, 128], [1, w]])
                P.dma("pool", l1.t[:, kc, c0:c0 + w], src, writes=[l1.b])
        for (nm, slot, r0, rows) in (("rwkv_w2", 0, 0, 64), ("rwkv_a2", 1, 0, 64), ("rwkv_g2", 2, 0, 128), ("rwkv_g2", 3, 128, 32)) + ((("rwkv_v2", 4, 0, 32),) if vres else ()):
            jj = j - 1 if nm == "rwkv_v2" else j
            src = AP(T.get(nm, jj), r0 * D, [[D, rows], [1, D]])
            P.dma("pool", l2.t[0:rows, slot, :], src, writes=[l2.b])
        for i, nm in enumerate(("rwkv_w0", "rwkv_a0", "rwkv_v0", "rwkv_k_k", "rwkv_k_a", "rwkv_r_k")):
            if nm == "rwkv_v0":
                if not vres:
                    continue
                P.dma("sp", bc.t[:, i, :], bcast_rows(T.get(nm, j - 1), 0, D), writes=[bc.b])
            else:
                P.dma("sp", bc.t[:, i, :], bcast_rows(T.get(nm, j), 0, D), writes=[bc.b])
        for i in range(6):
            src = AP(T.get("rwkv_mix", j), i * D, [[1, 128], [128, 8]])
            P.dma("sp", mixT.t[:, i, :], src, writes=[mixT.b], allow_slow_non_contiguous=True)
        P.dma("sp", ident.t[:, :], T.get("ident").ap(), writes=[ident.b])
        P.dma("sp", tri2.t[:, :], T.get("tri2").ap(), writes=[tri2.b])

        nblk = 0
        for b in range(NSEQ):
            for blk in range(S // TBR):
                t0 = blk * TBR
                if blk == 0:
                    P.op("pool", lambda e: e.memset(xTf.t[:, :, 0:1], 0.0), writes=[xTf.b])
                else:
                    P.op("pool", lambda e: e.tensor_copy(out=xTf.t[:, :, 0:1], in_=xTf.t[:, :, TBR:TBR + 1]),
                         reads=[xTf.b], writes=[xTf.b])
                for tt in range(NT):
                    Xt = Xs[tt % 2]
                    P.dma("sp", Xt.t[:, :], tok_ap(xin, b, t0 + tt * 128, S),
                          reads=[xin["bufs"][b][t0 // 128 + tt]], writes=[Xt.b])
                    for g4 in range(2):
                        for q in range(4):
                            kc = g4 * 4 + q
                            P.op("pe", lambda e, kc=kc, q=q, Xt=Xt: e.transpose(
                                ptr.t[:, q * 128:(q + 1) * 128], Xt.t[:, kc * 128:(kc + 1) * 128], ident.t[:, :]),
                                reads=[Xt.b, ident.b], writes=[ptr.b])
                        P.op("act", lambda e, g4=g4, tt=tt: e.activation(
                            out=xTf.t[:, g4 * 4:(g4 + 1) * 4, 1 + tt * 128:1 + (tt + 1) * 128],
                            in_=ptr.t[:, :].rearrange("p (a b) -> p a b", a=4), func=AF.Copy),
                            reads=[ptr.b], writes=[xTf.b])
                P.op("dve", lambda e: e.tensor_tensor(out=xxT.t[:, :, :], in0=xTf.t[:, :, 0:TBR], in1=xTf.t[:, :, 1:TBR + 1],
                                                      op=ALU.subtract), reads=[xTf.b], writes=[xxT.b])

                def make_xm(i, slot):
                    tm, xo = tmpm[slot], xm[slot]
                    P.op("pool", lambda e: e.tensor_tensor(
                        out=tm.t[:, :, :], in0=xxT.t[:, :, :], in1=mixT.t[:, i, :].unsqueeze(2).to_broadcast([128, 8, TBR]),
                        op=ALU.mult), reads=[xxT.b, mixT.b], writes=[tm.b])
                    P.op("dve", lambda e: e.tensor_tensor(out=xo.t[:, :, :], in0=tm.t[:, :, :], in1=xTf.t[:, :, 1:TBR + 1],
                                                          op=ALU.add), reads=[tm.b, xTf.b], writes=[xo.b])
                    return xo

                def proj_full(xo, br, dst_key):
                    for tt in range(NT):
                        pr = pp2[tt % 2]
                        for hf in range(2):
                            for kc in range(8):
                                P.op("pe", lambda e, kc=kc, hf=hf, tt=tt, pr=pr: e.matmul(
                                    pr[hf].t[:, :], xo.t[:, kc, tt * 128:(tt + 1) * 128],
                                    wrkv.t[:, br * 8 + kc, hf * 512:(hf + 1) * 512], start=(kc == 0), stop=(kc == 7)),
                                    reads=[xo.b, wrkv.b], writes=[pr[hf].b])
                            dst = tl[dst_key][tt]
                            P.op("act", lambda e, hf=hf, pr=pr, dst=dst: e.activation(
                                out=dst.t[:, hf * 512:(hf + 1) * 512], in_=pr[hf].t[:, :], func=AF.Copy),
                                reads=[pr[hf].b], writes=[dst.b])

                def lora1(xo, c0, w, key, func, pi):
                    pq = pl[pi]
                    for kc in range(8):
                        P.op("pe", lambda e, kc=kc: e.matmul(pq.t[0:w, 0:TBR], l1.t[:, kc, c0:c0 + w], xo.t[:, kc, :],
                                                             start=(kc == 0), stop=(kc == 7)),
                             reads=[xo.b, l1.b], writes=[pq.b])
                    P.op("act", lambda e: e.activation(out=hl[key].t[0:w, :], in_=pq.t[0:w, 0:TBR], func=func),
                         reads=[pq.b], writes=[hl[key].b])

                def lora2(parts, dst_key):
                    for tt in range(NT):
                        pr = pp2[tt % 2]
                        for hf in range(2):
                            for n, (key, rows, slot) in enumerate(parts):
                                P.op("pe", lambda e, key=key, rows=rows, slot=slot, n=n, hf=hf, tt=tt, pr=pr: e.matmul(
                                    pr[hf].t[:, :], hl[key].t[0:rows, tt * 128:(tt + 1) * 128],
                                    l2.t[0:rows, slot, hf * 512:(hf + 1) * 512], start=(n == 0), stop=(n == len(parts) - 1)),
                                    reads=[hl[key].b, l2.b], writes=[pr[hf].b])
                            dst = tl[dst_key][tt]
                            P.op("act", lambda e, hf=hf, pr=pr, dst=dst: e.activation(
                                out=dst.t[:, hf * 512:(hf + 1) * 512], in_=pr[hf].t[:, :], func=AF.Copy),
                                reads=[pr[hf].b], writes=[dst.b])

                xo = make_xm(0, 0); proj_full(xo, 0, "R")
                xo = make_xm(1, 1); proj_full(xo, 1, "K")
                xo = make_xm(2, 0); proj_full(xo, 2, "V")
                if vres:
                    lora1(xo, 288, 32, "v", AF.Copy, 0)
                    lora2([("v", 32, 4)], "ZV")
                xo = make_xm(3, 1); lora1(xo, 0, 64, "w", AF.Tanh, 1); lora2([("w", 64, 0)], "ZW")
                xo = make_xm(4, 0); lora1(xo, 64, 64, "a", AF.Copy, 0); lora2([("a", 64, 1)], "ZA")
                xo = make_xm(5, 1)
                lora1(xo, 128, 128, "g0", AF.Sigmoid, 1)
                lora1(xo, 256, 32, "g1", AF.Sigmoid, 0)
                lora2([("g0", 128, 2), ("g1", 32, 3)], "G")

                for tt in range(NT):
                    R, Kt, V, ZW, ZA, G, ZV, KK, TMP, BT, VF = [tl[k][tt] for k in names]
                    tb = xin["bufs"]
                    tix = t0 // 128 + tt
                    v3 = lambda t: t.t[:, :].rearrange("p (h n) -> p h n", h=16)
                    P.op("dve", lambda e: e.tensor_tensor(out=ZW.t[:, :], in0=ZW.t[:, :], in1=bc.t[:, 0, :], op=ALU.add),
                         reads=[ZW.b, bc.b], writes=[ZW.b])
                    P.op("act", lambda e: e.activation(out=ZW.t[:, :], in_=ZW.t[:, :], func=AF.Sigmoid), reads=[ZW.b], writes=[ZW.b])
                    P.op("dve", lambda e: e.tensor_tensor(out=ZA.t[:, :], in0=ZA.t[:, :], in1=bc.t[:, 1, :], op=ALU.add),
                         reads=[ZA.b, bc.b], writes=[ZA.b])
                    P.op("act", lambda e: e.activation(out=ZA.t[:, :], in_=ZA.t[:, :], func=AF.Sigmoid), reads=[ZA.b], writes=[ZA.b])
                    P.op("dve", lambda e: e.tensor_tensor(out=KK.t[:, :], in0=Kt.t[:, :], in1=bc.t[:, 3, :], op=ALU.mult),
                         reads=[Kt.b, bc.b], writes=[KK.b])
                    P.op("act", lambda e: e.activation(out=TMP.t[:, :], in_=KK.t[:, :], func=AF.Square), reads=[KK.b], writes=[TMP.b])
                    P.op("dve", lambda e: e.tensor_reduce(out=ss.t[:, :], in_=v3(TMP), axis=AX.X, op=ALU.add),
                         reads=[TMP.b], writes=[ss.b])
                    P.op("act", lambda e: e.activation(out=ss.t[:, :], in_=ss.t[:, :], func=AF.Sqrt), reads=[ss.b], writes=[ss.b])
                    P.op("dve", lambda e: e.tensor_scalar_max(out=ss.t[:, :], in0=ss.t[:, :], scalar1=1e-12), reads=[ss.b], writes=[ss.b])
                    P.op("dve", lambda e: e.reciprocal(out=ss.t[:, :], in_=ss.t[:, :]), reads=[ss.b], writes=[ss.b])
                    P.op("dve", lambda e: e.tensor_tensor(out=v3(KK), in0=v3(KK), in1=ss.t[:, :].unsqueeze(2).to_broadcast([128, 16, 64]),
                                                          op=ALU.mult), reads=[KK.b, ss.b], writes=[KK.b])
                    P.op("pool", lambda e: e.tensor_tensor(out=BT.t[:, :], in0=KK.t[:, :], in1=ZA.t[:, :], op=ALU.mult),
                         reads=[KK.b, ZA.b], writes=[BT.b])
                    P.op("dve", lambda e: e.scalar_tensor_tensor(out=TMP.t[:, :], in0=ZA.t[:, :], scalar=-1.0, in1=bc.t[:, 4, :],
                                                                 op0=ALU.add, op1=ALU.mult), reads=[ZA.b, bc.b, TMP.b], writes=[TMP.b])
                    P.op("dve", lambda e: e.scalar_tensor_tensor(out=Kt.t[:, :], in0=TMP.t[:, :], scalar=1.0, in1=Kt.t[:, :],
                                                                 op0=ALU.add, op1=ALU.mult), reads=[TMP.b, Kt.b], writes=[Kt.b])
                    P.op("act", lambda e: e.mul(out=KK.t[:, :], in_=KK.t[:, :], mul=-1.0) if False else
                         e.activation(out=KK.t[:, :], in_=KK.t[:, :], func=AF.Copy, scale=-1.0), reads=[KK.b, BT.b], writes=[KK.b])
                    if vres:
                        P.dma("sp", VF.t[:, :], tok_ap(SC["vfirst"], b, t0 + tt * 128, S),
                              reads=[SC["vfirst"]["bufs"][b][tix]], writes=[VF.b])
                        P.op("dve", lambda e: e.tensor_tensor(out=ZV.t[:, :], in0=ZV.t[:, :], in1=bc.t[:, 2, :], op=ALU.add),
                             reads=[ZV.b, bc.b], writes=[ZV.b])
                        P.op("act", lambda e: e.activation(out=ZV.t[:, :], in_=ZV.t[:, :], func=AF.Sigmoid), reads=[ZV.b], writes=[ZV.b])
                        P.op("pool", lambda e: e.tensor_tensor(out=VF.t[:, :], in0=VF.t[:, :], in1=V.t[:, :], op=ALU.subtract),
                             reads=[VF.b, V.b], writes=[VF.b])
                        P.op("pool", lambda e: e.tensor_tensor(out=VF.t[:, :], in0=VF.t[:, :], in1=ZV.t[:, :], op=ALU.mult),
                             reads=[VF.b, ZV.b], writes=[VF.b])
                        P.op("dve", lambda e: e.tensor_tensor(out=V.t[:, :], in0=V.t[:, :], in1=VF.t[:, :], op=ALU.add),
                             reads=[V.b, VF.b], writes=[V.b])
                    else:
                        P.dma("sp", tok_ap(SC["vfirst"], b, t0 + tt * 128, S), V.t[:, :], reads=[V.b],
                              writes=[SC["vfirst"]["bufs"][b][tix]])
                    P.dma("sp", tok_ap(SC["v"], b, t0 + tt * 128, S), V.t[:, :], reads=[V.b], writes=[SC["v"]["bufs"][b][tix]])
                    P.dma("sp", tok_ap(SC["g"], b, t0 + tt * 128, S), G.t[:, :], reads=[G.b], writes=[SC["g"]["bufs"][b][tix]])
                    P.op("dve", lambda e: e.tensor_tensor(out=TMP.t[:, :], in0=R.t[:, :], in1=Kt.t[:, :], op=ALU.mult),
                         reads=[R.b, Kt.b, TMP.b], writes=[TMP.b])
                    P.op("pool", lambda e: e.tensor_tensor(out=TMP.t[:, :], in0=TMP.t[:, :], in1=bc.t[:, 5, :], op=ALU.mult),
                         reads=[TMP.b, bc.b], writes=[TMP.b])
                    P.op("dve", lambda e: e.tensor_reduce(out=bs.t[:, :], in_=v3(TMP), axis=AX.X, op=ALU.add),
                         reads=[TMP.b], writes=[bs.b])
                    P.op("dve", lambda e: e.tensor_tensor(out=v3(TMP), in0=v3(V), in1=bs.t[:, :].unsqueeze(2).to_broadcast([128, 16, 64]),
                                                          op=ALU.mult), reads=[V.b, bs.b, TMP.b], writes=[TMP.b])
                    P.dma("sp", tok_ap(SC["bonus"], b, t0 + tt * 128, S), TMP.t[:, :], reads=[TMP.b], writes=[SC["bonus"]["bufs"][b][tix]])
                    CD = math.exp(-0.5)
                    pcs = pp2[tt % 2]
                    for hf in range(2):
                        P.op("pe", lambda e, hf=hf: e.matmul(pcs[hf].t[:, :], tri2.t[:, :], ZW.t[:, hf * 512:(hf + 1) * 512], start=True, stop=True),
                             reads=[tri2.b, ZW.b], writes=[pcs[hf].b])
                        P.op("act", lambda e, hf=hf: e.activation(out=EP.t[:, hf * 512:(hf + 1) * 512], in_=pcs[hf].t[:, :], func=AF.Exp, scale=-CD),
                             reads=[pcs[hf].b, EP.b], writes=[EP.b])
                        P.op("act", lambda e, hf=hf: e.activation(out=EM.t[:, hf * 512:(hf + 1) * 512], in_=pcs[hf].t[:, :], func=AF.Exp, scale=CD),
                             reads=[pcs[hf].b, EM.b], writes=[EM.b])
                    P.op("act", lambda e: e.activation(out=ZW.t[:, :], in_=ZW.t[:, :], func=AF.Exp, scale=CD), reads=[ZW.b], writes=[ZW.b])
                    P.op("dve", lambda e: e.tensor_tensor(out=ZW.t[:, :], in0=ZW.t[:, :], in1=EP.t[:, :], op=ALU.mult), reads=[ZW.b, EP.b], writes=[ZW.b])
                    P.op("dve", lambda e: e.tensor_tensor(out=R.t[:, :], in0=R.t[:, :], in1=EP.t[:, :], op=ALU.mult), reads=[R.b, EP.b], writes=[R.b])
                    P.op("pool", lambda e: e.tensor_tensor(out=Kt.t[:, :], in0=Kt.t[:, :], in1=EM.t[:, :], op=ALU.mult), reads=[Kt.b, EM.b], writes=[Kt.b])
                    P.op("dve", lambda e: e.tensor_tensor(out=BT.t[:, :], in0=BT.t[:, :], in1=EM.t[:, :], op=ALU.mult), reads=[BT.b, EM.b], writes=[BT.b])
                    P.op("pool", lambda e: e.tensor_tensor(out=KK.t[:, :], in0=KK.t[:, :], in1=ZW.t[:, :], op=ALU.mult), reads=[KK.b, ZW.b], writes=[KK.b])
                    P.dma("sp", tok_ap(SC["k"], b, t0 + tt * 128, S), Kt.t[:, :], reads=[Kt.b], writes=[SC["k"]["bufs"][b][tix]])
                    P.dma("sp", tok_ap(SC["b"], b, t0 + tt * 128, S), BT.t[:, :], reads=[BT.b], writes=[SC["b"]["bufs"][b][tix]])
                    for cc in range(2):
                        dst = AP(SC["wc"]["t"], ((b * (S // 64)) + (t0 + tt * 128) // 64 + cc) * D, [[D, 1], [1, D]])
                        P.dma("sp", dst, EP.t[cc * 64 + 63:cc * 64 + 64, :], reads=[EP.b], writes=[SC["wc"]["bufs"][b][tix]])
                    for key, src_t in (("fa", KK), ("fb", BT), ("fk", Kt), ("fr", R)):
                        for g4 in range(2):
                            pq = pcs[g4]
                            for q in range(4):
                                hp = g4 * 4 + q
                                P.op("pe", lambda e, q=q, hp=hp, pq=pq, src_t=src_t: e.transpose(
                                    pq.t[:, q * 128:(q + 1) * 128], src_t.t[:, hp * 128:(hp + 1) * 128], ident.t[:, :]),
                                    reads=[src_t.b, ident.b], writes=[pq.b])
                            ft = FT[nft % 2]
                            nft += 1
                            P.op("act", lambda e, pq=pq, ft=ft: e.activation(out=ft.t[:, :, :], in_=pq.t[:, :].rearrange("p (a b) -> p a b", a=4), func=AF.Copy),
                                 reads=[pq.b, ft.b], writes=[ft.b])
                            dst = AP(SC[key]["t"], (b * 1024 + g4 * 512) * S + t0 + tt * 128, [[S, 128], [128 * S, 4], [1, 128]])
                            P.dma("sp", dst, ft.t[:, :, :], reads=[ft.b], writes=[SC[key]["bufs"][b][tix]])
        P.barrier()


def rwkv_scan_phase(P, cfg, SC, T):
    S, NSEQ = cfg.S, cfg.NSEQ
    C = 64
    NCH_ = S // C
    with ExitStack() as ph:
        FM = [P.sb(ph, "FM%d" % i, [64, 4, 16, 64], F32) for i in range(2)]
        TM = [P.sb(ph, "TM%d" % i, [64, 3, D], F32) for i in range(2)]
        WcT = [P.sb(ph, "WcT%d" % i, [64, 16], F32) for i in range(2)]
        S0T = P.sb(ph, "S0T", [64, 16, 64], F32, n=2)
        mk = P.sb(ph, "mk", [64, 3, 64], F32)
        idn = P.sb(ph, "idn", [128, 128], F32)
        A_ = [[P.sb(ph, "A%d%d" % (g, i), [64, 8, 64], F32) for i in range(2)] for g in range(2)]
        B_ = [[P.sb(ph, "B%d%d" % (g, i), [64, 8, 64], F32) for i in range(2)] for g in range(2)]
        X_ = [[P.sb(ph, "X%d%d" % (g, i), [64, 8, 64], F32) for i in range(2)] for g in range(2)]
        Mm = [P.sb(ph, "Mm%d" % g, [64, 8, 64], F32) for g in range(2)]
        Pb = [P.sb(ph, "Pb%d" % g, [64, 8, 64], F32) for g in range(2)]
        Pk = [P.sb(ph, "Pk%d" % g, [64, 8, 64], F32) for g in range(2)]
        Y = [P.sb(ph, "Ych%d" % i, [64, 16, 64], F32, n=2) for i in range(2)]
        pz = [[P.ps(ph, "pz%d%d" % (g, i)) for i in range(4)] for g in range(2)]
        P.dma("sp", mk.t[:, :, :], T.get("scan_masks").ap(), writes=[mk.b])
        P.dma("sp", idn.t[:, :], T.get("ident").ap(), writes=[idn.b])

        def load(b, c, slot):
            tix = (c * C) // 128
            for ti, key in enumerate(("fa", "fb", "fk", "fr")):
                src = AP(SC[key]["t"], b * 1024 * S + c * C, [[S, 64], [64 * S, 16], [1, 64]])
                P.dma("sp", FM[slot].t[:, ti, :, :], src, reads=[SC[key]["bufs"][b][tix]], writes=[FM[slot].b])
            for ti, key in enumerate(("v", "b", "k")):
                src = AP(SC[key]["t"], (b * S + c * C) * D, [[D, 64], [1, D]])
                P.dma("sp", TM[slot].t[:, ti, :], src, reads=[SC[key]["bufs"][b][tix]], writes=[TM[slot].b])
            src = AP(SC["wc"]["t"], (b * NCH_ + c) * D, [[1, 64], [64, 16]])
            P.dma("sp", WcT[slot].t[:, :], src, reads=[SC["wc"]["bufs"][b][tix]], writes=[WcT[slot].b], allow_slow_non_contiguous=True)

        def mm8(g, ps, lhs_fn, rhs_fn, start, stop, reads):
            for p in range(8):
                P.op("pe", lambda e, p=p: e.matmul(ps.t[0:64, p * 64:(p + 1) * 64], lhs_fn(p), rhs_fn(p), start=start, stop=stop),
                     reads=reads, writes=[ps.b])

        def evac_mask(g, dst, ps, mi):
            P.op("dve", lambda e: e.tensor_tensor(
                out=dst.t[:, :, :], in0=ps.t[0:64, :].rearrange("p (a b) -> p a b", a=8),
                in1=sap(mk.t, mi * 64, [[0, 8], [1, 64]], parts=64), op=ALU.mult), reads=[ps.b, mk.b, dst.b], writes=[dst.b])

        def evac_copy(g, dst_ap, dstb, ps):
            P.op("act", lambda e: e.activation(out=dst_ap, in_=ps.t[0:64, :].rearrange("p (a b) -> p a b", a=8), func=AF.Copy),
                 reads=[ps.b, dstb], writes=[dstb])

        order = [(b, c) for b in range(NSEQ) for c in range(NCH_)]
        load(order[0][0], order[0][1], 0)
        for n, (b, c) in enumerate(order):
            slot = n % 2
            if n + 1 < len(order):
                load(order[n + 1][0], order[n + 1][1], 1 - slot)
            fm, tm, wc = FM[slot], TM[slot], WcT[slot]
            if c == 0:
                for g in range(2):
                    P.op("pool", lambda e, g=g: e.memset(S0T.t[:, g * 8:(g + 1) * 8, :], 0.0), writes=[S0T.bs[g]])
            yb = Y[n % 2]
            f = lambda ti: (lambda g: (lambda p: fm.t[:, ti, g * 8 + p, :]))
            fa, fb, fk, fr = f(0), f(1), f(2), f(3)
            tmv = lambda ti: (lambda g: (lambda p: tm.t[:, ti, (g * 8 + p) * 64:(g * 8 + p + 1) * 64]))
            tv, tb, tk = tmv(0), tmv(1), tmv(2)
            s0 = lambda g: (lambda p: S0T.t[:, g * 8 + p, :])
            G = (0, 1)
            for g in G:
                mm8(g, pz[g][0], fb(g), fa(g), True, True, [fm.b])
                mm8(g, pz[g][1], fa(g), fb(g), True, True, [fm.b])
            for g in G:
                evac_mask(g, A_[g][0], pz[g][0], 0)
                evac_mask(g, B_[g][0], pz[g][1], 1)
            for g in G:
                mm8(g, pz[g][2], fk(g), fa(g), True, True, [fm.b])
                mm8(g, pz[g][3], fb(g), fr(g), True, True, [fm.b])
                mm8(g, pz[g][0], fk(g), fr(g), True, True, [fm.b])
            for g in G:
                evac_mask(g, Mm[g], pz[g][2], 0)
                evac_mask(g, Pb[g], pz[g][3], 2)
                evac_mask(g, Pk[g], pz[g][0], 2)
            for g in G:
                ps = pz[g][1]
                for p in range(8):
                    P.op("pe", lambda e, p=p, g=g, ps=ps: e.matmul(ps.t[0:64, p * 64:(p + 1) * 64], fa(g)(p), s0(g)(p), start=True, stop=False),
                         reads=[fm.b, S0T.bs[g]], writes=[ps.b])
                    P.op("pe", lambda e, p=p, g=g, ps=ps: e.matmul(ps.t[0:64, p * 64:(p + 1) * 64], Mm[g].t[:, p, :], tv(g)(p), start=False, stop=True),
                         reads=[Mm[g].b, tm.b], writes=[ps.b])
            for g in G:
                evac_copy(g, X_[g][0].t[:, :, :], X_[g][0].b, pz[g][1])
            for i in range(6):
                cur, nxt = i % 2, (i + 1) % 2
                for g in G:
                    mm8(g, pz[g][2], (lambda p, g=g: A_[g][cur].t[:, p, :]), (lambda p, g=g: X_[g][cur].t[:, p, :]), True, True,
                        [A_[g][cur].b, X_[g][cur].b])
                    if i < 5:
                        mm8(g, pz[g][0], (lambda p, g=g: B_[g][cur].t[:, p, :]), (lambda p, g=g: A_[g][cur].t[:, p, :]), True, True,
                            [A_[g][cur].b, B_[g][cur].b])
                        if i < 4:
                            mm8(g, pz[g][1], (lambda p, g=g: A_[g][cur].t[:, p, :]), (lambda p, g=g: B_[g][cur].t[:, p, :]), True, True,
                                [A_[g][cur].b, B_[g][cur].b])
                for g in G:
                    P.op("dve", lambda e, g=g: e.tensor_tensor(
                        out=X_[g][nxt].t[:, :, :], in0=pz[g][2].t[0:64, :].rearrange("p (a b) -> p a b", a=8), in1=X_[g][cur].t[:, :, :], op=ALU.add),
                        reads=[pz[g][2].b, X_[g][cur].b, X_[g][nxt].b], writes=[X_[g][nxt].b])
                    if i < 5:
                        evac_copy(g, A_[g][nxt].t[:, :, :], A_[g][nxt].b, pz[g][0])
                        if i < 4:
                            evac_copy(g, B_[g][nxt].t[:, :, :], B_[g][nxt].b, pz[g][1])
            UT = [X_[g][0] for g in G]
            for g in G:
                ps = pz[g][3]
                for p in range(8):
                    P.op("pe", lambda e, p=p, g=g, ps=ps: e.matmul(ps.t[0:64, p * 64:(p + 1) * 64], fr(g)(p), s0(g)(p), start=True, stop=False),
                         reads=[fm.b, S0T.bs[g]], writes=[ps.b])
                    P.op("pe", lambda e, p=p, g=g, ps=ps: e.matmul(ps.t[0:64, p * 64:(p + 1) * 64], Pb[g].t[:, p, :], UT[g].t[:, p, :], start=False, stop=False),
                         reads=[Pb[g].b, UT[g].b], writes=[ps.b])
                    P.op("pe", lambda e, p=p, g=g, ps=ps: e.matmul(ps.t[0:64, p * 64:(p + 1) * 64], Pk[g].t[:, p, :], tv(g)(p), start=False, stop=True),
                         reads=[Pk[g].b, tm.b], writes=[ps.b])
            for g in G:
                evac_copy(g, yb.t[:, g * 8:(g + 1) * 8, :], yb.bs[g], pz[g][3])
            for g in G:
                ps = pz[g][2]
                for p in range(8):
                    P.op("pe", lambda e, p=p, g=g, ps=ps: e.matmul(ps.t[0:64, p * 64:(p + 1) * 64], idn.t[0:64, 0:64], s0(g)(p), start=True, stop=False),
                         reads=[idn.b, S0T.bs[g]], writes=[ps.b])
                    P.op("pe", lambda e, p=p, g=g, ps=ps: e.matmul(ps.t[0:64, p * 64:(p + 1) * 64], tb(g)(p), UT[g].t[:, p, :], start=False, stop=False),
                         reads=[tm.b, UT[g].b], writes=[ps.b])
                    P.op("pe", lambda e, p=p, g=g, ps=ps: e.matmul(ps.t[0:64, p * 64:(p + 1) * 64], tk(g)(p), tv(g)(p), start=False, stop=True),
                         reads=[tm.b], writes=[ps.b])
            for g in G:
                P.op("dve", lambda e, g=g: e.tensor_tensor(
                    out=S0T.t[:, g * 8:(g + 1) * 8, :], in0=pz[g][2].t[0:64, :].rearrange("p (a b) -> p a b", a=8),
                    in1=wc.t[:, g * 8:(g + 1) * 8].unsqueeze(2).to_broadcast([64, 8, 64]), op=ALU.mult),
                    reads=[pz[g][2].b, wc.b, S0T.bs[g]], writes=[S0T.bs[g]])
            dst = AP(SC["y"]["t"], (b * S + c * C) * D, [[D, 64], [1, D]])
            P.dma("sp", dst, yb.t[:, :, :], reads=[yb.bs[0], yb.bs[1]], writes=[SC["y"]["bufs"][b][(c * C) // 128]])
        P.barrier()


def rwkv_post_phase(P, cfg, li, j, xin, xout, T, SC):
    S, NSEQ = cfg.S, cfg.NSEQ
    with ExitStack() as ph:
        wo = P.sb(ph, "wo", [128, 8, D], BF16)
        bc = P.sb(ph, "bc", [128, 2, D], F32)
        g_bc = P.sb(ph, "g_bc", [128, D], F32)
        b_bc = P.sb(ph, "b_bc", [128, D], F32)
        ident = P.sb(ph, "ident", [128, 128], F32)
        Y = [P.sb(ph, "Y%d" % i, [128, D], F32) for i in range(2)]
        BO = [P.sb(ph, "BO%d" % i, [128, D], F32) for i in range(2)]
        Gt = [P.sb(ph, "Gt%d" % i, [128, D], F32) for i in range(2)]
        Xr = [P.sb(ph, "Xr%d" % i, [128, D], F32) for i in range(2)]
        SQ = P.sb(ph, "SQ", [128, D], F32)
        zT = P.sb(ph, "zT", [128, 8, 128], BF16)
        s1 = P.sb(ph, "s1", [128, 16], F32)
        s2 = P.sb(ph, "s2", [128, 16], F32)
        m2 = P.sb(ph, "m2", [128, 16], F32)
        stats = P.sb(ph, "stats", [128, 2, 6], F32)
        mv = P.sb(ph, "mv", [128, 2], F32)
        rstd = P.sb(ph, "rstd", [128, 1], F32)
        ptr = P.ps(ph, "ptr", shape=(128, 1024), n=2)
        pdn = [P.ps(ph, "pdn%d" % i) for i in range(2)]
        load_weight_bf16(P, wo.t, wo.b, T.get("rwkv_w_o", j), 0, D, D, 8)
        P.dma("sp", bc.t[:, 0, :], bcast_rows(T.get("rwkv_lnx_g", j), 0, D), writes=[bc.b])
        P.dma("sp", bc.t[:, 1, :], bcast_rows(T.get("rwkv_lnx_b", j), 0, D), writes=[bc.b])
        P.dma("sp", g_bc.t[:, :], bcast_rows(T.get("ln_g", li), 0, D), writes=[g_bc.b])
        P.dma("sp", b_bc.t[:, :], bcast_rows(T.get("ln_b", li), 0, D), writes=[b_bc.b])
        P.dma("sp", ident.t[:, :], T.get("ident").ap(), writes=[ident.b])
        n = 0
        for b in range(NSEQ):
            for tix in range(S // 128):
                y, bo, gt, xr = Y[n % 2], BO[n % 2], Gt[n % 2], Xr[n % 2]
                n += 1
                t0 = tix * 128
                P.dma("sp", y.t[:, :], tok_ap(SC["y"], b, t0, S), reads=[SC["y"]["bufs"][b][tix]], writes=[y.b])
                P.dma("sp", bo.t[:, :], tok_ap(SC["bonus"], b, t0, S), reads=[SC["bonus"]["bufs"][b][tix]], writes=[bo.b])
                P.dma("sp", gt.t[:, :], tok_ap(SC["g"], b, t0, S), reads=[SC["g"]["bufs"][b][tix]], writes=[gt.b])
                P.dma("sp", xr.t[:, :], tok_ap(xin, b, t0, S), reads=[xin["bufs"][b][tix]], writes=[xr.b])
                y3 = y.t[:, :].rearrange("p (h n) -> p h n", h=16)
                P.op("dve", lambda e: e.tensor_reduce(out=s1.t[:, :], in_=y3, axis=AX.X, op=ALU.add), reads=[y.b], writes=[s1.b])
                P.op("act", lambda e: e.activation(out=SQ.t[:, :], in_=y.t[:, :], func=AF.Square), reads=[y.b], writes=[SQ.b])
                P.op("dve", lambda e: e.tensor_reduce(out=s2.t[:, :], in_=SQ.t[:, :].rearrange("p (h n) -> p h n", h=16),
                                                      axis=AX.X, op=ALU.add), reads=[SQ.b], writes=[s2.b])
                P.op("dve", lambda e: e.tensor_scalar(out=s1.t[:, :], in0=s1.t[:, :], scalar1=1.0 / 64, scalar2=None, op0=ALU.mult),
                     reads=[s1.b], writes=[s1.b])
                P.op("dve", lambda e: e.tensor_tensor(out=m2.t[:, :], in0=s1.t[:, :], in1=s1.t[:, :], op=ALU.mult),
                     reads=[s1.b], writes=[m2.b])
                P.op("dve", lambda e: e.scalar_tensor_tensor(out=s2.t[:, :], in0=s2.t[:, :], scalar=1.0 / 64, in1=m2.t[:, :],
                                                             op0=ALU.mult, op1=ALU.subtract), reads=[s2.b, m2.b], writes=[s2.b])
                P.op("act", lambda e: e.activation(out=s2.t[:, :], in_=s2.t[:, :], func=AF.Sqrt, bias=64e-5, scale=1.0),
                     reads=[s2.b], writes=[s2.b])
                P.op("dve", lambda e: e.reciprocal(out=s2.t[:, :], in_=s2.t[:, :]), reads=[s2.b], writes=[s2.b])
                bc16 = lambda t: t.t[:, :].unsqueeze(2).to_broadcast([128, 16, 64])
                P.op("dve", lambda e: e.tensor_tensor(out=y3, in0=y3, in1=bc16(s1), op=ALU.subtract), reads=[y.b, s1.b], writes=[y.b])
                P.op("dve", lambda e: e.tensor_tensor(out=y3, in0=y3, in1=bc16(s2), op=ALU.mult), reads=[y.b, s2.b], writes=[y.b])
                P.op("pool", lambda e: e.tensor_tensor(out=y.t[:, :], in0=y.t[:, :], in1=bc.t[:, 0, :], op=ALU.mult),
                     reads=[y.b, bc.b], writes=[y.b])
                P.op("pool", lambda e: e.tensor_tensor(out=y.t[:, :], in0=y.t[:, :], in1=bc.t[:, 1, :], op=ALU.add),
                     reads=[y.b, bc.b], writes=[y.b])
                P.op("dve", lambda e: e.tensor_tensor(out=y.t[:, :], in0=y.t[:, :], in1=bo.t[:, :], op=ALU.add),
                     reads=[y.b, bo.b], writes=[y.b])
                P.op("dve", lambda e: e.tensor_tensor(out=y.t[:, :], in0=y.t[:, :], in1=gt.t[:, :], op=ALU.mult),
                     reads=[y.b, gt.b], writes=[y.b])
                outproj_ln_tile(P, y.t[:, :], y.b, xr.t[:, :], xr.b, wo, zT, ptr, pdn, ident, stats, mv, rstd, g_bc, b_bc)
                P.dma("sp", tok_ap(xout, b, t0, S), xr.t[:, :], reads=[xr.b], writes=[xout["bufs"][b][tix]])
        P.barrier()


NEG = -1.0e30


def t5_bucket_np(n):
    n = np.asarray(n, dtype=np.int64)
    nf = np.maximum(n, 1).astype(np.float32)
    large = 16 + (np.log(nf / np.float32(16)) / np.float32(math.log(128 / 16)) * np.float32(16)).astype(np.int32)
    large = np.minimum(large, 31)
    return np.where(n < 16, n, large)


def host_consts():
    oh = np.zeros((32, 384), np.float32)
    for q in range(256):
        oh[int(t5_bucket_np(255 - q)), q] = 1.0
    cm = np.zeros((128, 128), np.float32)
    cm[np.triu_indices(128, 1)] = NEG
    ii, tt_ = np.meshgrid(np.arange(128), np.arange(128), indexing="ij")
    tri2 = ((ii // 64 == tt_ // 64) & (ii <= tt_)).astype(np.float32)
    s_, t_ = np.meshgrid(np.arange(64), np.arange(64), indexing="ij")
    masks = np.stack([(s_ < t_), (s_ > t_), (s_ <= t_)], 1).astype(np.float32)
    return {"ident": np.eye(128, dtype=np.float32), "bucket_oh": oh, "causal_mask": cm, "tri2": tri2, "scan_masks": masks,
            "pow2": np.tile((2.0 ** -np.arange(32, dtype=np.float64)).astype(np.float32)[None, :], (128, 1))}


def dsa_phase(P, cfg, li, j, xin, xout, T, VD):
    S, NSEQ = cfg.S, cfg.NSEQ
    NCK = S // 128
    qk_scale = 64 ** -0.5
    with ExitStack() as ph:
        w_in = P.sb(ph, "w_in", [128, 8, 456], BF16)
        w_uq = P.sb(ph, "w_uq", [128, 2, D], BF16)
        w_qi = P.sb(ph, "w_qi", [128, 2, 512], BF16)
        w_uk = P.sb(ph, "w_uk", [128, 8, 128], BF16)
        w_uv = P.sb(ph, "w_uv", [128, 16, 64], BF16)
        wo = P.sb(ph, "wo", [128, 8, D], BF16)
        gq = P.sb(ph, "gq", [128, 256], F32)
        gkv = P.sb(ph, "gkv", [128, 128], F32)
        gki = P.sb(ph, "gki", [128, 2, 64], F32)
        g_bc = P.sb(ph, "g_bc", [128, D], F32)
        b_bc = P.sb(ph, "b_bc", [128, D], F32)
        ident = P.sb(ph, "ident", [128, 128], F32)
        identb = P.sb(ph, "identb", [128, 128], BF16)
        cmask = P.sb(ph, "cmask", [128, 128], F32)
        TB = P.sb(ph, "TB", [128, 16, 2, 128], F32)
        rb31 = P.sb(ph, "rb31", [128, 16], F32)
        rbs = P.sb(ph, "rbs", [32, 16], F32)
        ohs = P.sb(ph, "ohs", [32, 384], F32)
        vrs = P.sb(ph, "vrs", [16, 384], F32)
        ckvT = P.sb(ph, "ckvT", [128, S], BF16)
        ckv = P.sb(ph, "ckv", [128, NCK, 128], BF16)
        kiT = P.sb(ph, "kiT", [128, S], BF16)
        Xr = [P.sb(ph, "Xr%d" % i, [128, D], F32) for i in range(2)]
        xT = P.sb(ph, "xT", [128, 8, 128], BF16)
        HS = P.sb(ph, "HS", [128, 456], F32)
        CQ = P.sb(ph, "CQ", [128, 512], F32)
        WI = P.sb(ph, "WI", [128, 8], F32)
        sm = P.sb(ph, "sm", [128, 8], F32)
        cqT = P.sb(ph, "cqT", [128, 2, 128], BF16)
        qT = P.sb(ph, "qT", [128, 8, 128], BF16)
        qabsT = P.sb(ph, "qabsT", [128, 16, 128], BF16)
        qabsT2 = P.sb(ph, "qabsT2", [128, 16, 128], BF16)
        qidxT = P.sb(ph, "qidxT", [128, 4, 128], BF16)
        score = P.sb(ph, "score", [128, S], F32)
        work = P.sb(ph, "work", [128, S], F32)
        mask = P.sb(ph, "mask", [128, S], F32)
        mask2 = P.sb(ph, "mask2", [128, S], F32)
        relu = [P.sb(ph, "relu%d" % i, [128, 512], F32) for i in range(2)]
        m8 = P.sb(ph, "m8", [128, 8], F32)
        junk = P.sb(ph, "junk", [128, S], BF16)
        pw = P.sb(ph, "pw", [128, 32], F32)
        wtab = P.sb(ph, "wtab", [128, 32], F32)
        bw = P.sb(ph, "bw", [128, 1], F32)
        blo = P.sb(ph, "blo", [128, 1], F32)
        bnb = P.sb(ph, "bnb", [128, 1], F32)
        bss = P.sb(ph, "bss", [128, 1], F32)
        Lh = [P.sb(ph, "Lh%d" % i, [128, S], F32) for i in range(2)]
        Pm = [P.sb(ph, "Pm%d" % i, [128, S], BF16) for i in range(2)]
        pT = [P.sb(ph, "pT%d" % i, [128, NCK, 128], BF16) for i in range(2)]
        olT = [P.sb(ph, "olT%d" % i, [128, 128], BF16) for i in range(2)]
        mx = P.sb(ph, "mx", [128, 16], F32)
        rs = P.sb(ph, "rs", [128, 16], F32)
        O = P.sb(ph, "O", [128, D], F32)
        zT = P.sb(ph, "zT", [128, 8, 128], BF16)
        stats = P.sb(ph, "stats", [128, 2, 6], F32)
        mv = P.sb(ph, "mv", [128, 2], F32)
        rstd = P.sb(ph, "rstd", [128, 1], F32)
        ptr = P.ps(ph, "ptr", shape=(128, 1024), n=2)
        pb2 = P.ps(ph, "pb2")
        scg = [P.ps(ph, "scg%d" % i) for i in range(2)]
        ppT = P.ps(ph, "ppT", shape=(128, 1024), dt=BF16)
        po = P.ps(ph, "po", shape=(128, 1024))

        load_weight_bf16(P, w_in.t, w_in.b, T.get("dsa_w_in", j), 0, D, 456, 8)
        load_weight_bf16(P, w_uq.t, w_uq.b, T.get("dsa_w_uq", j), 0, 256, D, 2)
        load_weight_bf16(P, w_qi.t, w_qi.b, T.get("dsa_w_qidx", j), 0, 256, 512, 2)
        load_weight_bf16(P, wo.t, wo.b, T.get("dsa_w_o", j), 0, D, D, 8)
        for hp in range(8):
            src = AP(T.get("dsa_w_uk", j), hp * 2 * 64 * 128, [[128, 128], [1, 128]])
            P.dma("pool", w_uk.t[:, hp, :], src, writes=[w_uk.b])
        for h in range(16):
            src = AP(T.get("dsa_w_uv", j), h * 128 * 64, [[64, 128], [1, 64]])
            P.dma("pool", w_uv.t[:, h, :], src, writes=[w_uv.b])
        P.dma("sp", gq.t[:, :], bcast_rows(T.get("dsa_q_norm_g", j), 0, 256), writes=[gq.b])
        P.dma("sp", gkv.t[:, :], bcast_rows(T.get("dsa_kv_norm_g", j), 0, 128), writes=[gkv.b])
        P.dma("sp", gki.t[:, 0, :], bcast_rows(T.get("dsa_kidx_g", j), 0, 64), writes=[gki.b])
        P.dma("sp", gki.t[:, 1, :], bcast_rows(T.get("dsa_kidx_b", j), 0, 64), writes=[gki.b])
        P.dma("sp", g_bc.t[:, :], bcast_rows(T.get("ln_g", li), 0, D), writes=[g_bc.b])
        P.dma("sp", b_bc.t[:, :], bcast_rows(T.get("ln_b", li), 0, D), writes=[b_bc.b])
        P.dma("sp", ident.t[:, :], T.get("ident").ap(), writes=[ident.b])
        P.dma("sp", cmask.t[:, :], T.get("causal_mask").ap(), writes=[cmask.b])
        P.dma("sp", pw.t[:, :], T.get("pow2").ap(), writes=[pw.b])
        P.dma("sp", rb31.t[:, :], bcast_rows(T.get("rel_bias"), 31 * 16, 16), writes=[rb31.b])
        P.dma("sp", rbs.t[:, :], T.get("rel_bias").ap(), writes=[rbs.b])
        P.dma("sp", ohs.t[:, :], T.get("bucket_oh").ap(), writes=[ohs.b])
        P.op("act", lambda e: e.activation(out=identb.t[:, :], in_=ident.t[:, :], func=AF.Copy), reads=[ident.b], writes=[identb.b])
        P.op("pe", lambda e: e.matmul(pb2.t[0:16, 0:384], rbs.t[:, :], ohs.t[:, :], start=True, stop=True),
             reads=[rbs.b, ohs.b], writes=[pb2.b])
        P.op("act", lambda e: e.activation(out=vrs.t[:, :], in_=pb2.t[0:16, 0:384], func=AF.Copy), reads=[pb2.b], writes=[vrs.b])
        vdb = VD["bufs"][0][0]
        P.dma("sp", VD["t"].ap(), vrs.t[:, :], reads=[vrs.b], writes=[vdb])
        for ti in range(128):
            for d in range(2):
                src = AP(VD["t"], 255 - d * 128 - ti, [[0, 1], [384, 16], [1, 128]])
                P.dma("sp", TB.t[ti:ti + 1, :, d, :], src, reads=[vdb], writes=[TB.b])

        qab = [qabsT, qabsT2]
        msk = [mask, mask2]

        def stage_PI(b, i):
            t0 = i * 128
            L = (i + 1) * 128
            xr = Xr[i % 2]
            qabsT = qab[i % 2]
            P.dma("sp", xr.t[:, :], tok_ap(xin, b, t0, S), reads=[xin["bufs"][b][i]], writes=[xr.b])
            for kc in range(8):
                P.op("pe", lambda e, kc=kc: e.transpose(ptr.t[:, kc * 128:(kc + 1) * 128], xr.t[:, kc * 128:(kc + 1) * 128], ident.t[:, :]),
                     reads=[xr.b, ident.b], writes=[ptr.bs[kc // 4]])
                if kc % 4 == 3:
                    g4 = kc // 4
                    P.op("act", lambda e, g4=g4: e.activation(
                        out=xT.t[:, g4 * 4:(g4 + 1) * 4, :], in_=ptr.t[:, g4 * 512:(g4 + 1) * 512].rearrange("p (a b) -> p a b", a=4),
                        func=AF.Copy), reads=[ptr.bs[g4]], writes=[xT.b])
            for kc in range(8):
                P.op("pe", lambda e, kc=kc: e.matmul(pb2.t[:, 0:456], xT.t[:, kc, :], w_in.t[:, kc, :], start=(kc == 0), stop=(kc == 7)),
                     reads=[xT.b, w_in.b], writes=[pb2.b])
            P.op("act", lambda e: e.activation(out=HS.t[:, :], in_=pb2.t[:, 0:456], func=AF.Copy), reads=[pb2.b], writes=[HS.b])
            for (c0, w, gt, col) in ((0, 256, gq, 0), (256, 128, gkv, 1)):
                P.op("act", lambda e, c0=c0, w=w, col=col: e.activation(
                    out=work.t[:, 0:w], in_=HS.t[:, c0:c0 + w], func=AF.Square, accum_out=sm.t[:, col:col + 1]),
                    reads=[HS.b, work.b], writes=[work.b, sm.b])
                P.op("act", lambda e, w=w, col=col: e.activation(
                    out=sm.t[:, col:col + 1], in_=sm.t[:, col:col + 1], func=AF.Sqrt, bias=1e-6, scale=1.0 / w),
                    reads=[sm.b], writes=[sm.b])
                P.op("dve", lambda e, col=col: e.reciprocal(out=sm.t[:, col:col + 1], in_=sm.t[:, col:col + 1]), reads=[sm.b], writes=[sm.b])
                P.op("dve", lambda e, c0=c0, w=w, gt=gt, col=col: e.scalar_tensor_tensor(
                    out=CQ.t[:, c0:c0 + w], in0=HS.t[:, c0:c0 + w], scalar=sm.t[:, col:col + 1], in1=gt.t[:, :],
                    op0=ALU.mult, op1=ALU.mult), reads=[HS.b, sm.b, gt.b, CQ.b], writes=[CQ.b])
            P.op("dve", lambda e: e.bn_stats(out=stats.t[:, 0, :], in_=HS.t[:, 384:448]), reads=[HS.b], writes=[stats.b])
            P.op("dve", lambda e: e.bn_aggr(out=mv.t[:, :], in_=stats.t[:, 0, :]), reads=[stats.b], writes=[mv.b])
            P.op("act", lambda e: e.activation(out=rstd.t[:, :], in_=mv.t[:, 1:2], func=AF.Sqrt, bias=LN_EPS, scale=1.0),
                 reads=[mv.b], writes=[rstd.b])
            P.op("dve", lambda e: e.reciprocal(out=rstd.t[:, :], in_=rstd.t[:, :]), reads=[rstd.b], writes=[rstd.b])
            P.op("dve", lambda e: e.tensor_scalar(out=CQ.t[:, 384:448], in0=HS.t[:, 384:448], scalar1=mv.t[:, 0:1], scalar2=rstd.t[:, 0:1],
                                                  op0=ALU.subtract, op1=ALU.mult), reads=[HS.b, mv.b, rstd.b, CQ.b], writes=[CQ.b])
            P.op("dve", lambda e: e.tensor_tensor(out=CQ.t[:, 384:448], in0=CQ.t[:, 384:448], in1=gki.t[:, 0, :], op=ALU.mult),
                 reads=[CQ.b, gki.b], writes=[CQ.b])
            P.op("dve", lambda e: e.tensor_tensor(out=CQ.t[:, 384:448], in0=CQ.t[:, 384:448], in1=gki.t[:, 1, :], op=ALU.add),
                 reads=[CQ.b, gki.b], writes=[CQ.b])
            P.op("dve", lambda e: e.tensor_copy(out=CQ.t[:, 448:512], in_=CQ.t[:, 384:448]), reads=[CQ.b], writes=[CQ.b])
            P.op("dve", lambda e: e.tensor_scalar(out=WI.t[:, :], in0=HS.t[:, 448:456], scalar1=512 ** -0.5, scalar2=None, op0=ALU.mult),
                 reads=[HS.b], writes=[WI.b])
            P.op("act", lambda e, i=i: e.activation(out=ckv.t[:, i, :], in_=CQ.t[:, 256:384], func=AF.Copy), reads=[CQ.b], writes=[ckv.b])
            for q in range(4):
                P.op("pe", lambda e, q=q: e.transpose(ptr.t[:, q * 128:(q + 1) * 128], CQ.t[:, q * 128:(q + 1) * 128], ident.t[:, :]),
                     reads=[CQ.b, ident.b], writes=[ptr.bs[0]])
            P.op("act", lambda e: e.activation(out=cqT.t[:, :, :], in_=ptr.t[:, 0:256].rearrange("p (a b) -> p a b", a=2), func=AF.Copy),
                 reads=[ptr.bs[0]], writes=[cqT.b])
            P.op("act", lambda e, t0=t0: e.activation(out=ckvT.t[:, t0:t0 + 128], in_=ptr.t[:, 256:384], func=AF.Copy),
                 reads=[ptr.bs[0]], writes=[ckvT.b])
            P.op("act", lambda e, t0=t0: e.activation(out=kiT.t[:, t0:t0 + 128], in_=ptr.t[:, 384:512], func=AF.Copy),
                 reads=[ptr.bs[0]], writes=[kiT.b])
            for g4 in range(2):
                for q in range(4):
                    hp = g4 * 4 + q
                    for kc in range(2):
                        P.op("pe", lambda e, hp=hp, q=q, kc=kc: e.matmul(
                            ptr.t[:, 512 + q * 128:512 + (q + 1) * 128], w_uq.t[:, kc, hp * 128:(hp + 1) * 128], cqT.t[:, kc, :],
                            start=(kc == 0), stop=(kc == 1)), reads=[w_uq.b, cqT.b], writes=[ptr.bs[1]])
                P.op("act", lambda e, g4=g4: e.activation(
                    out=qT.t[:, g4 * 4:(g4 + 1) * 4, :], in_=ptr.t[:, 512:1024].rearrange("p (a b) -> p a b", a=4), func=AF.Copy),
                    reads=[ptr.bs[1]], writes=[qT.b])
            for q in range(4):
                for kc in range(2):
                    P.op("pe", lambda e, q=q, kc=kc: e.matmul(
                        ptr.t[:, q * 128:(q + 1) * 128], w_qi.t[:, kc, q * 128:(q + 1) * 128], cqT.t[:, kc, :],
                        start=(kc == 0), stop=(kc == 1)), reads=[w_qi.b, cqT.b], writes=[ptr.bs[0]])
            P.op("act", lambda e: e.activation(out=qidxT.t[:, :, :], in_=ptr.t[:, 0:512].rearrange("p (a b) -> p a b", a=4), func=AF.Copy),
                 reads=[ptr.bs[0]], writes=[qidxT.b])
            for g8 in range(2):
                for q in range(4):
                    for par in range(2):
                        h = g8 * 8 + q * 2 + par
                        p0 = par * 64
                        P.op("pe", lambda e, h=h, q=q, p0=p0, par=par: e.matmul(
                            ptr.t[:, par * 512 + q * 128:par * 512 + (q + 1) * 128], w_uk.t[p0:p0 + 64, h // 2, :],
                            qT.t[p0:p0 + 64, h // 2, :], start=True, stop=True), reads=[w_uk.b, qT.b], writes=[ptr.bs[par]])
                for par in range(2):
                    P.op("act", lambda e, g8=g8, par=par: e.activation(
                        out=sap(qabsT.t, (g8 * 8 + par) * 128, [[256, 4], [1, 128]]),
                        in_=ptr.t[:, par * 512:(par + 1) * 512].rearrange("p (a b) -> p a b", a=4), func=AF.Copy),
                        reads=[ptr.bs[par]], writes=[qabsT.b])
            nkb = (L + 511) // 512
            for kb in range(nkb):
                c0 = kb * 512
                cw_ = min(512, L - c0)
                for hi in range(8):
                    p0 = (hi % 2) * 64
                    pq = scg[hi % 2]
                    rl = relu[hi % 2]
                    P.op("pe", lambda e, hi=hi, p0=p0, pq=pq, c0=c0, cw_=cw_: e.matmul(
                        pq.t[:, 0:cw_], qidxT.t[p0:p0 + 64, hi // 2, :], kiT.t[p0:p0 + 64, c0:c0 + cw_], start=True, stop=True),
                        reads=[qidxT.b, kiT.b], writes=[pq.b])
                    P.op("act", lambda e, pq=pq, rl=rl, cw_=cw_: e.activation(out=rl.t[:, 0:cw_], in_=pq.t[:, 0:cw_], func=AF.Relu),
                         reads=[pq.b], writes=[rl.b])
                    if hi == 0:
                        P.op("dve", lambda e, rl=rl, c0=c0, cw_=cw_: e.tensor_scalar(
                            out=score.t[:, c0:c0 + cw_], in0=rl.t[:, 0:cw_], scalar1=WI.t[:, 0:1], scalar2=None, op0=ALU.mult),
                            reads=[rl.b, WI.b, score.b], writes=[score.b])
                    else:
                        P.op("dve", lambda e, rl=rl, c0=c0, cw_=cw_, hi=hi: e.scalar_tensor_tensor(
                            out=score.t[:, c0:c0 + cw_], in0=rl.t[:, 0:cw_], scalar=WI.t[:, hi:hi + 1], in1=score.t[:, c0:c0 + cw_],
                            op0=ALU.mult, op1=ALU.add), reads=[rl.b, WI.b, score.b], writes=[score.b])
            P.op("dve", lambda e, t0=t0: e.tensor_tensor(out=score.t[:, t0:t0 + 128], in0=score.t[:, t0:t0 + 128], in1=cmask.t[:, :], op=ALU.add),
                 reads=[score.b, cmask.b], writes=[score.b])

        def topk_ops(b, i):
            L = (i + 1) * 128
            mask = msk[i % 2]
            ops = []
            if L <= 256:
                ops.append(lambda: P.op("dve", lambda e: e.tensor_scalar(out=mask.t[:, 0:L], in0=score.t[:, 0:L], scalar1=-1.0e29, scalar2=None, op0=ALU.is_ge),
                                        reads=[score.b, mask.b], writes=[mask.b]))
                return ops
            NIT = 26
            ops.append(lambda: P.op("dve", lambda e: e.tensor_scalar(out=work.t[:, 0:L], in0=score.t[:, 0:L], scalar1=-1.0e29, scalar2=0.0,
                                                                     op0=ALU.is_ge, op1=ALU.add), reads=[score.b, work.b], writes=[work.b]))
            ops.append(lambda: P.op("dve", lambda e: e.tensor_tensor(out=work.t[:, 0:L], in0=work.t[:, 0:L], in1=score.t[:, 0:L], op=ALU.mult),
                                    reads=[score.b, work.b], writes=[work.b]))
            ops.append(lambda: P.op("dve", lambda e: e.tensor_reduce(out=bw.t[:, 0:1], in_=work.t[:, 0:L], axis=AX.X, op=ALU.max, apply_absolute_value=True),
                                    reads=[work.b, bw.b], writes=[bw.b]))
            ops.append(lambda: P.op("dve", lambda e: e.tensor_scalar(out=bw.t[:, 0:1], in0=bw.t[:, 0:1], scalar1=1.0, scalar2=None, op0=ALU.add),
                                    reads=[bw.b], writes=[bw.b]))
            ops.append(lambda: P.op("dve", lambda e: e.tensor_scalar(out=wtab.t[:, :], in0=pw.t[:, :], scalar1=bw.t[:, 0:1], scalar2=None, op0=ALU.mult),
                                    reads=[pw.b, bw.b, wtab.b], writes=[wtab.b]))
            ops.append(lambda: P.op("dve", lambda e: e.tensor_scalar(out=blo.t[:, :], in0=bw.t[:, 0:1], scalar1=-1.0, scalar2=None, op0=ALU.mult),
                                    reads=[bw.b, blo.b], writes=[blo.b]))
            ops.append(lambda: P.op("dve", lambda e: e.scalar_tensor_tensor(out=bnb.t[:, :], in0=blo.t[:, :], scalar=-1.0, in1=wtab.t[:, 0:1],
                                                                           op0=ALU.mult, op1=ALU.subtract), reads=[blo.b, wtab.b, bnb.b], writes=[bnb.b]))
            for n in range(NIT):
                ops.append(lambda: P.op("act", lambda e: e.activation(out=junk.t[:, 0:L], in_=score.t[:, 0:L], func=AF.Sign, bias=bnb.t[:, 0:1], scale=1.0,
                                                                      accum_out=bss.t[:, 0:1]), reads=[score.b, bnb.b, junk.b, bss.b], writes=[junk.b, bss.b]))
                ops.append(lambda: P.op("dve", lambda e: e.tensor_scalar(out=bss.t[:, 0:1], in0=bss.t[:, 0:1], scalar1=float(511 - L) - 0.25, scalar2=None, op0=ALU.is_ge),
                                        reads=[bss.b], writes=[bss.b]))
                ops.append(lambda n=n: P.op("dve", lambda e: e.scalar_tensor_tensor(out=blo.t[:, :], in0=bss.t[:, 0:1], scalar=wtab.t[:, n:n + 1], in1=blo.t[:, :],
                                                                                   op0=ALU.mult, op1=ALU.add), reads=[bss.b, wtab.b, blo.b], writes=[blo.b]))
                if n + 1 < NIT:
                    ops.append(lambda n=n: P.op("dve", lambda e: e.scalar_tensor_tensor(out=bnb.t[:, :], in0=blo.t[:, :], scalar=-1.0, in1=wtab.t[:, n + 1:n + 2],
                                                                                       op0=ALU.mult, op1=ALU.subtract), reads=[blo.b, wtab.b, bnb.b], writes=[bnb.b]))
            ops.append(lambda: P.op("dve", lambda e: e.tensor_scalar(out=mask.t[:, 0:L], in0=score.t[:, 0:L], scalar1=blo.t[:, 0:1], scalar2=None, op0=ALU.is_ge),
                                    reads=[score.b, blo.b, mask.b], writes=[mask.b]))
            return ops

        def stage_H(b, i, side):
            t0 = i * 128
            L = (i + 1) * 128
            nkb = (L + 511) // 512
            qabsT = qab[i % 2]
            mask = msk[i % 2]
            per_head = (len(side) + 15) // 16
            nfar = max(0, i - 1)
            def head_A(h):
                lh, pm = Lh[h % 2], Pm[h % 2]
                for kb in range(nkb):
                    c0 = kb * 512
                    cw_ = min(512, L - c0)
                    pq = scg[kb % 2]
                    P.op("pe", lambda e, h=h, pq=pq, c0=c0, cw_=cw_: e.matmul(
                        pq.t[:, 0:cw_], qabsT.t[:, h, :], ckvT.t[:, c0:c0 + cw_], start=True, stop=True),
                        reads=[qabsT.b, ckvT.b], writes=[pq.b])
                    for ck in range(c0 // 128, (c0 + cw_) // 128):
                        lo = ck * 128 - c0
                        if ck < nfar:
                            continue
                        d = i - ck
                        P.op("dve", lambda e, pq=pq, lo=lo, ck=ck, d=d, h=h, lh=lh: e.scalar_tensor_tensor(
                            out=lh.t[:, ck * 128:(ck + 1) * 128], in0=pq.t[:, lo:lo + 128], scalar=qk_scale, in1=TB.t[:, h, d, :],
                            op0=ALU.mult, op1=ALU.add), reads=[pq.b, TB.b, lh.b], writes=[lh.b])
                    f1 = min(c0 + cw_, nfar * 128)
                    if f1 > c0:
                        P.op("dve", lambda e, pq=pq, c0=c0, f1=f1, h=h, lh=lh: e.tensor_scalar(
                            out=lh.t[:, c0:f1], in0=pq.t[:, 0:f1 - c0], scalar1=qk_scale, scalar2=rb31.t[:, h:h + 1],
                            op0=ALU.mult, op1=ALU.add), reads=[pq.b, rb31.b, lh.b], writes=[lh.b])
                P.op("dve", lambda e, h=h, lh=lh, L=L: e.tensor_reduce(out=mx.t[:, h:h + 1], in_=lh.t[:, 0:L], axis=AX.X, op=ALU.max, negate=True),
                     reads=[lh.b, mx.b], writes=[mx.b])
                P.op("act", lambda e, h=h, lh=lh, L=L: e.activation(out=lh.t[:, 0:L], in_=lh.t[:, 0:L], func=AF.Exp, bias=mx.t[:, h:h + 1], scale=1.0),
                     reads=[lh.b, mx.b], writes=[lh.b])
                P.op("dve", lambda e, h=h, lh=lh, pm=pm, L=L: e.scalar_tensor_tensor(
                    out=pm.t[:, 0:L], in0=lh.t[:, 0:L], scalar=1.0, in1=mask.t[:, 0:L], op0=ALU.mult, op1=ALU.mult,
                    accum_out=rs.t[:, h:h + 1]), reads=[lh.b, mask.b, pm.b, rs.b], writes=[pm.b, rs.b])

            def head_B(h):
                pm, pt, ol = Pm[h % 2], pT[h % 2], olT[h % 2]
                for ck in range(i + 1):
                    P.op("pe", lambda e, ck=ck, pm=pm: e.transpose(ppT.t[:, (ck % 8) * 128:(ck % 8 + 1) * 128], pm.t[:, ck * 128:(ck + 1) * 128], identb.t[:, :]),
                         reads=[pm.b, identb.b], writes=[ppT.b])
                    if ck % 8 == 7 or ck == i:
                        c8 = (ck // 8) * 8
                        nn = ck - c8 + 1
                        P.op("act", lambda e, c8=c8, nn=nn, pt=pt: e.activation(
                            out=pt.t[:, c8:c8 + nn, :], in_=ppT.t[:, 0:nn * 128].rearrange("p (a b) -> p a b", a=nn), func=AF.Copy),
                            reads=[ppT.b, pt.b], writes=[pt.b])
                for ck in range(i + 1):
                    P.op("pe", lambda e, ck=ck, pt=pt: e.matmul(pb2.t[:, 0:128], ckv.t[:, ck, :], pt.t[:, ck, :], start=(ck == 0), stop=(ck == i)),
                         reads=[ckv.b, pt.b], writes=[pb2.b])
                P.op("act", lambda e, ol=ol: e.activation(out=ol.t[:, :], in_=pb2.t[:, 0:128], func=AF.Copy), reads=[pb2.b, ol.b], writes=[ol.b])
                P.op("pe", lambda e, h=h, ol=ol: e.matmul(po.t[:, h * 64:(h + 1) * 64], ol.t[:, :], w_uv.t[:, h, :], start=True, stop=True),
                     reads=[ol.b, w_uv.b], writes=[po.b])

            head_A(0)
            for h in range(16):
                if h + 1 < 16:
                    head_A(h + 1)
                head_B(h)
                for _ in range(per_head):
                    if side:
                        side.pop(0)()
            while side:
                side.pop(0)()

        def stage_O(b, i):
            t0 = i * 128
            xr = Xr[i % 2]
            P.op("dve", lambda e: e.reciprocal(out=rs.t[:, :], in_=rs.t[:, :]), reads=[rs.b], writes=[rs.b])
            for hf in range(2):
                P.op("dve", lambda e, hf=hf: e.tensor_tensor(
                    out=O.t[:, hf * 512:(hf + 1) * 512].rearrange("p (h n) -> p h n", h=8),
                    in0=po.t[:, hf * 512:(hf + 1) * 512].rearrange("p (h n) -> p h n", h=8),
                    in1=rs.t[:, hf * 8:(hf + 1) * 8].unsqueeze(2).to_broadcast([128, 8, 64]), op=ALU.mult),
                    reads=[po.b, rs.b, O.b], writes=[O.b])
            outproj_ln_tile(P, O.t[:, :], O.b, xr.t[:, :], xr.b, wo, zT, ptr, scg, ident, stats, mv, rstd, g_bc, b_bc)
            P.dma("sp", tok_ap(xout, b, t0, S), xr.t[:, :], reads=[xr.b], writes=[xout["bufs"][b][i]])

        for b in range(NSEQ):
            stage_PI(b, 0)
            for f in topk_ops(b, 0):
                f()
            for i in range(NCK):
                side = []
                if i + 1 < NCK:
                    stage_PI(b, i + 1)
                    side = topk_ops(b, i + 1)
                stage_H(b, i, side)
                stage_O(b, i)
        P.barrier()


INPUT_SPECS = [
    ("x", None), ("ln_g", (4, 2, 1024)), ("ln_b", (4, 2, 1024)), ("rwkv_mix", (2, 6, 1024)),
    ("rwkv_w_rkv", (2, 3, 1024, 1024)), ("rwkv_w0", (2, 1024)), ("rwkv_w1", (2, 1024, 64)),
    ("rwkv_w2", (2, 64, 1024)), ("rwkv_a0", (2, 1024)), ("rwkv_a1", (2, 1024, 64)), ("rwkv_a2", (2, 64, 1024)),
    ("rwkv_v0", (1, 1024)), ("rwkv_v1", (1, 1024, 32)), ("rwkv_v2", (1, 32, 1024)), ("rwkv_g1", (2, 1024, 160)),
    ("rwkv_g2", (2, 160, 1024)), ("rwkv_k_k", (2, 1024)), ("rwkv_k_a", (2, 1024)), ("rwkv_r_k", (2, 16, 64)),
    ("rwkv_lnx_g", (2, 1024)), ("rwkv_lnx_b", (2, 1024)), ("rwkv_w_o", (2, 1024, 1024)),
    ("dsa_w_in", (2, 1024, 456)), ("dsa_q_norm_g", (2, 256)), ("dsa_kv_norm_g", (2, 128)),
    ("dsa_w_uq", (2, 256, 1024)), ("dsa_w_uk", (2, 16, 64, 128)), ("dsa_w_uv", (2, 16, 128, 64)),
    ("dsa_w_qidx", (2, 256, 512)), ("dsa_kidx_g", (2, 64)), ("dsa_kidx_b", (2, 64)), ("dsa_w_o", (2, 1024, 1024)),
    ("rel_bias", (32, 16)), ("ffn_w_up", (4, 1024, 5632)), ("ffn_conv_w", (4, 3, 5632)), ("ffn_conv_b", (4, 5632)),
    ("ffn_w_down", (4, 2816, 1024)),
]


CONST_SHAPES = {"ident": (128, 128), "bucket_oh": (32, 384), "causal_mask": (128, 128), "tri2": (128, 128), "scan_masks": (64, 3, 64), "pow2": (128, 32)}


class Tensors:
    def __init__(self, nc):
        self.nc = nc
        self.d = {}
        self.shapes = dict(INPUT_SPECS)

    def get(self, name, j=None):
        key = name if j is None else "%s_%d" % (name, j)
        if key not in self.d:
            if name in CONST_SHAPES:
                shp = list(CONST_SHAPES[name])
            else:
                shp = list(self.shapes[name])
                if j is not None:
                    shp = shp[1:]
            self.d[key] = self.nc.dram_tensor(key, shp, F32, kind="ExternalInput")
        return self.d[key]


def host_inputs(names, inputs):
    out = {}
    consts = host_consts()
    for key in names:
        if key in consts:
            out[key] = consts[key]
        elif key in inputs:
            out[key] = np.ascontiguousarray(inputs[key], dtype=np.float32)
        else:
            name, j = key.rsplit("_", 1)
            out[key] = np.ascontiguousarray(inputs[name][int(j)], dtype=np.float32)
    return out


def dram_stream(nc, name, cfg, kind="Internal"):
    t = nc.dram_tensor(name, [cfg.NSEQ, cfg.S, D], F32, kind=kind)
    return {"t": t, "bufs": [[Buf() for _ in range(cfg.S // 128)] for _ in range(cfg.NSEQ)]}


def sc_stream(nc, name, cfg, kind="Internal"):
    t = nc.dram_tensor(name, [cfg.NSEQ * cfg.S + 1, D], F32, kind=kind)
    return {"t": t, "bufs": [[Buf() for _ in range(cfg.S // 128)] for _ in range(cfg.NSEQ)]}


def build(cfg):
    nc = bass.Bass("TRN2", target_bir_lowering=False)
    T = Tensors(nc)
    dbg = getattr(cfg, "debug_out", "all")
    xin = dram_stream(nc, "x", cfg, kind="ExternalInput")
    out = dram_stream(nc, "out", cfg, kind="ExternalOutput")
    stages = []
    for li in cfg.layers:
        if "mix" in cfg.parts:
            stages.append(("mix", li))
        if "ffn" in cfg.parts:
            stages.append(("ffn", li))
    with ExitStack() as es:
        P = Prog(nc, es)
        cur = xin
        SC = None
        VD = None
        for si, (kind, li) in enumerate(stages):
            last = si == len(stages) - 1
            nxt = out if last else dram_stream(nc, "xs%d" % si, cfg)
            if kind == "ffn":
                ffn_phase(P, cfg, li, cur, nxt, T)
            elif li % 2 == 0:
                j = li // 2
                if SC is None:
                    dk = "ExternalOutput" if dbg in ("pre", "scan") else "Internal"
                    SC = {k: sc_stream(nc, ("dbg_" + k) if dk == "ExternalOutput" else ("sc_" + k), cfg, kind=dk)
                          for k in ("k", "v", "b", "g", "bonus", "y", "fa", "fb", "fk", "fr", "wc")}
                    if j > 0:
                        SC["vfirst"] = sc_stream(nc, "vfirst_in", cfg, kind="ExternalInput")
                        T.d["vfirst_in"] = SC["vfirst"]["t"]
                    else:
                        SC["vfirst"] = sc_stream(nc, "sc_vfirst", cfg)
                rwkv_pre_phase(P, cfg, li, j, cur, T, SC)
                if dbg == "pre":
                    break
                rwkv_scan_phase(P, cfg, SC, T)
                if dbg == "scan":
                    break
                rwkv_post_phase(P, cfg, li, j, cur, nxt, T, SC)
            else:
                if VD is None:
                    VD = {"t": nc.dram_tensor("sc_vr", [16, 384], F32), "bufs": [[Buf()]]}
                dsa_phase(P, cfg, li, li // 2, cur, nxt, T, VD)
            cur = nxt
        P.barrier()
        print("instructions emitted:", P.ninst, "sems:", P.nsem, flush=True)
    return nc, list(T.d.keys())


def kernel(**inputs):
    cfg = Cfg()
    nc, names = build(cfg)
    x = np.ascontiguousarray(inputs["x"], dtype=np.float32)
    ncores = 8
    shards = np.split(x, ncores, axis=0)
    base = host_inputs(names, inputs)
    in_maps = []
    for c in range(ncores):
        m = dict(base)
        m["x"] = shards[c]
        in_maps.append(m)
    res = run_bass_kernel_spmd(nc, in_maps, core_ids=list(range(ncores)))
    return np.concatenate([r["out"] for r in res.results], axis=0)
```
